# Optimizing a Trainium2 kernel written in Bass

```python
import math
import jax, jax.numpy as jnp
from jax import lax
import numpy as np

D_MODEL = 2048
BATCH = 8
SEQ = 4096
DEPTH = 4

GRID_W = 64
CTX_LEN = 256
N_MIXERS = 3
N_LAYERS_GLA = (DEPTH + 2) // 3
N_LAYERS_MLA = (DEPTH + 1) // 3
N_LAYERS_SWA = DEPTH // 3
EPS = 1e-6
ROPE_THETA = 10000.0
ROPE_DIM = 64
BLOCK = 128
D_FF = -(-8 * D_MODEL // (3 * 256)) * 256
GLA_HEADS = 4
GLA_DK = D_MODEL // 2 // GLA_HEADS
GLA_DV = D_MODEL // GLA_HEADS
GLA_QK = GLA_HEADS * GLA_DK
GLA_VD = GLA_HEADS * GLA_DV
GLA_GATE_RANK = 16
GLA_TAU = 16.0
GLA_CHUNK = 64
MLA_HEADS = D_MODEL // 128
MLA_Q_RANK = 512
MLA_KV_RANK = 512
MLA_NOPE = 128
MLA_ROPE = ROPE_DIM
MLA_V = 128
MLA_SCALE = (MLA_NOPE + MLA_ROPE) ** -0.5
SWA_HEADS = D_MODEL // 64
SWA_KV_HEADS = SWA_HEADS // 8
SWA_GROUP = SWA_HEADS // SWA_KV_HEADS
SWA_HEAD_DIM = 64
SWA_SCALE = SWA_HEAD_DIM ** -0.5
WINDOW = 128

kernel_name = "hybrid_gla_mla_swa_dit_trunk"


def rmsnorm(x, g):
    xf = x.astype(jnp.float32)
    y = xf * lax.rsqrt(jnp.mean(xf * xf, axis=-1, keepdims=True) + EPS)
    return (y * g.astype(jnp.float32)).astype(x.dtype)


def modulation(cond, w_ada, b_ada):
    m = jax.nn.silu(cond) @ w_ada + b_ada
    return jnp.split(m, 6, axis=-1)


def adaln_in(h, g, shift, scale):
    return rmsnorm(h, g) * (1 + scale) + shift


def axial_rope_tables(n_rows):
    row = jnp.repeat(jnp.arange(n_rows), GRID_W).astype(jnp.float32)
    col = jnp.tile(jnp.arange(GRID_W), n_rows).astype(jnp.float32)
    n_freq = ROPE_DIM // 4
    inv_freq = ROPE_THETA ** (-jnp.arange(n_freq, dtype=jnp.float32) / n_freq)
    ang = jnp.concatenate([row[:, None] * inv_freq, col[:, None] * inv_freq], axis=-1)
    return jnp.cos(ang), jnp.sin(ang)


def apply_rope(x, cos, sin):
    x1, x2 = jnp.split(x, 2, axis=-1)
    cos = cos.astype(x.dtype)
    sin = sin.astype(x.dtype)
    return jnp.concatenate([x1 * cos - x2 * sin, x2 * cos + x1 * sin], axis=-1)


def _split_heads(a, n):
    return a.reshape(a.shape[:-1] + (n, a.shape[-1] // n))


def _to_blocks(a):
    return jnp.moveaxis(a.reshape((a.shape[0], a.shape[1] // BLOCK, BLOCK) + a.shape[2:]), 1, 0)


def _from_blocks(o):
    o = jnp.moveaxis(o, 0, 1)
    return o.reshape(o.shape[0], o.shape[1] * o.shape[2], -1)


def swiglu(h, w_in, w_out):
    g, u = jnp.split(h @ w_in, 2, axis=-1)
    return (jax.nn.silu(g) * u) @ w_out


def gla_chunked(q, k, v, lg, s0):
    out_dtype = v.dtype
    causal = jnp.tril(jnp.ones((GLA_CHUNK, GLA_CHUNK), bool))

    def to_chunks(a):
        b_, h_, t_, d_ = a.shape
        return jnp.moveaxis(a.reshape(b_, h_, t_ // GLA_CHUNK, GLA_CHUNK, d_), 2, 0)

    def step(s, inp):
        qc, kc, vc, gc = [t.astype(jnp.float32) for t in inp]
        b = jnp.cumsum(gc, axis=2)
        o_inter = jnp.einsum('bhcd,bhde->bhce', qc * jnp.exp(b), s)
        diff = jnp.where(causal[:, :, None], b[:, :, :, None, :] - b[:, :, None, :, :], -jnp.inf)
        decay = jnp.exp(diff)
        attn = jnp.einsum('bhijd,bhjd->bhij', qc[:, :, :, None, :] * decay, kc)
        o_intra = jnp.einsum('bhij,bhje->bhie', attn, vc)
        b_last = b[:, :, -1:, :]
        k_dec = kc * jnp.exp(b_last - b)
        s_new = jnp.exp(b_last[:, :, 0, :])[..., None] * s + jnp.einsum('bhcd,bhce->bhde', k_dec, vc)
        return s_new, o_inter + o_intra

    s_fin, o = lax.scan(step, s0, (to_chunks(q), to_chunks(k), to_chunks(v), to_chunks(lg)))
    o = jnp.moveaxis(o, 0, 2)
    o = o.reshape(o.shape[0], o.shape[1], -1, o.shape[-1]).astype(out_dtype)
    return o, s_fin


def gla_mixer(h_ctx, h_lat, w_in, w_gate_down, w_gate_up, b_gate, g_head, w_out, with_ctx_out):
    def heads(a):
        return jnp.swapaxes(_split_heads(a, GLA_HEADS), 1, 2)

    def project(h):
        q, k, v, r = jnp.split(h @ w_in, [GLA_QK, 2 * GLA_QK, 2 * GLA_QK + GLA_VD], axis=-1)
        lg = [heads(jax.nn.log_sigmoid(((h @ w_gate_down[d]) @ w_gate_up[d] + b_gate[d]).astype(jnp.float32)) / GLA_TAU)
              for d in range(2)]
        return heads(q) * (GLA_DK ** -0.5), heads(k), heads(v), r, lg

    qc, kc, vc, rc, lgc = project(h_ctx)
    ql, kl, vl, rl, lgl = project(h_lat)
    s0 = jnp.zeros(qc.shape[:2] + (GLA_DK, GLA_DV), jnp.float32)
    flip = lambda a: jnp.flip(a, axis=2)
    oc_f, sc_f = gla_chunked(qc, kc, vc, lgc[0], s0)
    ol_f, _ = gla_chunked(ql, kl, vl, lgl[0], sc_f)
    oc_b, sc_b = gla_chunked(flip(qc), flip(kc), flip(vc), flip(lgc[1]), s0)
    ol_b, _ = gla_chunked(flip(ql), flip(kl), flip(vl), flip(lgl[1]), sc_b)

    def out(o_f, o_b, r):
        o = rmsnorm(o_f + flip(o_b), g_head)
        o = jnp.swapaxes(o, 1, 2).reshape(r.shape[:-1] + (GLA_VD,))
        return (o * jax.nn.silu(r)) @ w_out

    y_lat = out(ol_f, ol_b, rl)
    y_ctx = out(oc_f, oc_b, rc) if with_ctx_out else None
    return y_ctx, y_lat


def mla_mixer(h_ctx, h_lat, cos, sin, w_in, g_q, w_uq, g_kv, w_ukv, w_out, with_ctx_out):
    def project(h, rotate):
        cq, ckv, k_rope = jnp.split(h @ w_in, [MLA_Q_RANK, MLA_Q_RANK + MLA_KV_RANK], axis=-1)
        q = _split_heads(rmsnorm(cq, g_q) @ w_uq, MLA_HEADS)
        kv = _split_heads(rmsnorm(ckv, g_kv) @ w_ukv, MLA_HEADS)
        q_nope, q_rope = jnp.split(q, [MLA_NOPE], axis=-1)
        k_nope, v = jnp.split(kv, [MLA_NOPE], axis=-1)
        if rotate:
            q_rope = apply_rope(q_rope, cos[:, None, :], sin[:, None, :])
            k_rope = apply_rope(k_rope, cos, sin)
        return q_nope, q_rope, k_nope, k_rope, v

    def attend(qn, qr, kn, kr, v):
        s = jnp.einsum('bqhd,bkhd->bhqk', qn, kn) + jnp.einsum('bqhr,bkr->bhqk', qr, kr)
        p = jax.nn.softmax(s.astype(jnp.float32) * MLA_SCALE, axis=-1).astype(v.dtype)
        return jnp.einsum('bhqk,bkhd->bqhd', p, v)

    qn_c, qr_c, kn_c, kr_c, v_c = project(h_ctx, False)
    qn_l, qr_l, kn_l, kr_l, v_l = project(h_lat, True)
    kn_all = jnp.concatenate([kn_c, kn_l], axis=1)
    kr_all = jnp.concatenate([kr_c, kr_l], axis=1)
    v_all = jnp.concatenate([v_c, v_l], axis=1)
    o = lax.map(lambda qb: attend(qb[0], qb[1], kn_all, kr_all, v_all), (_to_blocks(qn_l), _to_blocks(qr_l)))
    y_lat = _from_blocks(o) @ w_out
    if with_ctx_out:
        o_c = attend(qn_c, qr_c, kn_c, kr_c, v_c)
        y_ctx = o_c.reshape(o_c.shape[0], o_c.shape[1], -1) @ w_out
    else:
        y_ctx = None
    return y_ctx, y_lat


def swa_mixer(h_ctx, h_lat, cos, sin, w_in, sinks, w_out, with_ctx_out):
    def project(h, rotate):
        q, k, v = jnp.split(h @ w_in, [SWA_HEADS * SWA_HEAD_DIM, (SWA_HEADS + SWA_KV_HEADS) * SWA_HEAD_DIM], axis=-1)
        q = q.reshape(q.shape[:-1] + (SWA_KV_HEADS, SWA_GROUP, SWA_HEAD_DIM))
        k = _split_heads(k, SWA_KV_HEADS)
        v = _split_heads(v, SWA_KV_HEADS)
        if rotate:
            q = apply_rope(q, cos[:, None, None, :], sin[:, None, None, :])
            k = apply_rope(k, cos[:, None, :], sin[:, None, :])
        return q, k, v

    sink = sinks.reshape(SWA_KV_HEADS, SWA_GROUP).astype(jnp.float32)

    def attend(q, k, v, mask):
        s = jnp.einsum('bqngd,bknd->bngqk', q, k).astype(jnp.float32) * SWA_SCALE
        if mask is not None:
            s = jnp.where(mask, s, -jnp.inf)
        sink_col = jnp.broadcast_to(sink[None, :, :, None, None], s.shape[:-1] + (1,))
        p = jax.nn.softmax(jnp.concatenate([s, sink_col], axis=-1), axis=-1)[..., :-1].astype(v.dtype)
        return jnp.einsum('bngqk,bknd->bqngd', p, v)

    q_c, k_c, v_c = project(h_ctx, False)
    q_l, k_l, v_l = project(h_lat, True)
    t_lat = h_lat.shape[1]
    pad = ((0, 0), (BLOCK, BLOCK), (0, 0), (0, 0))
    k_pad = jnp.pad(k_l, pad)
    v_pad = jnp.pad(v_l, pad)
    a_idx = jnp.arange(BLOCK)
    b_idx = jnp.arange(3 * BLOCK)
    band = jnp.abs(a_idx[:, None] - b_idx[None, :] + BLOCK) <= WINDOW
    ctx_cols = jnp.ones((BLOCK, k_c.shape[1]), bool)

    def block(args):
        qb, n = args
        kb = lax.dynamic_slice_in_dim(k_pad, n * BLOCK, 3 * BLOCK, axis=1)
        vb = lax.dynamic_slice_in_dim(v_pad, n * BLOCK, 3 * BLOCK, axis=1)
        kpos = (n - 1) * BLOCK + b_idx
        valid = band & ((kpos >= 0) & (kpos < t_lat))[None, :]
        mask = jnp.concatenate([ctx_cols, valid], axis=1)
        return attend(qb, jnp.concatenate([k_c, kb], axis=1), jnp.concatenate([v_c, vb], axis=1), mask)

    o = lax.map(block, (_to_blocks(q_l), jnp.arange(t_lat // BLOCK)))
    y_lat = _from_blocks(o) @ w_out
    if with_ctx_out:
        o_c = attend(q_c, k_c, v_c, None)
        y_ctx = o_c.reshape(o_c.shape[0], o_c.shape[1], -1) @ w_out
    else:
        y_ctx = None
    return y_ctx, y_lat


def setup_inputs(seed: int = 0) -> dict:
    key = jax.random.key(seed)
    ks = iter(jax.random.split(key, 32))
    D = D_MODEL

    def nrm(shape, scale=1.0):
        return jax.random.normal(next(ks), shape, jnp.float32) * scale

    def gain(shape):
        return 1.0 + nrm(shape, 0.02)

    return {
        "x": nrm((BATCH, SEQ, D)),
        "c": nrm((BATCH, D)),
        "ctx": nrm((BATCH, CTX_LEN, D)),
        "c_ctx": nrm((D,)),
        "w_ada": nrm((DEPTH, D, 6 * D), 0.5 * D ** -0.5),
        "b_ada": nrm((DEPTH, 6 * D), 0.01),
        "g_norm": gain((DEPTH, 4, D)),
        "w_ffn_in": nrm((DEPTH, D, 2 * D_FF), D ** -0.5),
        "w_ffn_out": nrm((DEPTH, D_FF, D), D_FF ** -0.5),
        "gla_w_in": nrm((N_LAYERS_GLA, D, 2 * GLA_QK + 2 * GLA_VD), D ** -0.5),
        "gla_w_gate_down": nrm((N_LAYERS_GLA, 2, D, GLA_GATE_RANK), D ** -0.5),
        "gla_w_gate_up": nrm((N_LAYERS_GLA, 2, GLA_GATE_RANK, GLA_QK), GLA_GATE_RANK ** -0.5),
        "gla_b_gate": nrm((N_LAYERS_GLA, 2, GLA_QK), 0.1),
        "gla_g_head": gain((N_LAYERS_GLA, GLA_DV)),
        "gla_w_out": nrm((N_LAYERS_GLA, GLA_VD, D), GLA_VD ** -0.5),
        "mla_w_in": nrm((N_LAYERS_MLA, D, MLA_Q_RANK + MLA_KV_RANK + MLA_ROPE), D ** -0.5),
        "mla_g_q": gain((N_LAYERS_MLA, MLA_Q_RANK)),
        "mla_w_uq": nrm((N_LAYERS_MLA, MLA_Q_RANK, MLA_HEADS * (MLA_NOPE + MLA_ROPE)), MLA_Q_RANK ** -0.5),
        "mla_g_kv": gain((N_LAYERS_MLA, MLA_KV_RANK)),
        "mla_w_ukv": nrm((N_LAYERS_MLA, MLA_KV_RANK, MLA_HEADS * (MLA_NOPE + MLA_V)), MLA_KV_RANK ** -0.5),
        "mla_w_out": nrm((N_LAYERS_MLA, MLA_HEADS * MLA_V, D), (MLA_HEADS * MLA_V) ** -0.5),
        "swa_w_in": nrm((N_LAYERS_SWA, D, (SWA_HEADS + 2 * SWA_KV_HEADS) * SWA_HEAD_DIM), D ** -0.5),
        "swa_sinks": nrm((N_LAYERS_SWA, SWA_HEADS)),
        "swa_w_out": nrm((N_LAYERS_SWA, SWA_HEADS * SWA_HEAD_DIM, D), (SWA_HEADS * SWA_HEAD_DIM) ** -0.5),
    }


def reference(x, c, ctx, c_ctx, w_ada, b_ada, g_norm, w_ffn_in, w_ffn_out,
              gla_w_in, gla_w_gate_down, gla_w_gate_up, gla_b_gate, gla_g_head, gla_w_out,
              mla_w_in, mla_g_q, mla_w_uq, mla_g_kv, mla_w_ukv, mla_w_out,
              swa_w_in, swa_sinks, swa_w_out):
    ROWS = x.shape[1] // GRID_W
    cos, sin = axial_rope_tables(ROWS)
    h_lat, h_ctx = x, ctx
    for i in range(DEPTH):
        kind, j = i % N_MIXERS, i // N_MIXERS
        last = i == DEPTH - 1
        ml = [t[:, None, :] for t in modulation(c, w_ada[i], b_ada[i])]
        mc = modulation(c_ctx, w_ada[i], b_ada[i])
        a_lat = adaln_in(h_lat, g_norm[i, 0], ml[0], ml[1])
        a_ctx = adaln_in(h_ctx, g_norm[i, 0], mc[0], mc[1])
        if kind == 0:
            y_ctx, y_lat = gla_mixer(a_ctx, a_lat, gla_w_in[j], gla_w_gate_down[j], gla_w_gate_up[j],
                                     gla_b_gate[j], gla_g_head[j], gla_w_out[j], not last)
        elif kind == 1:
            y_ctx, y_lat = mla_mixer(a_ctx, a_lat, cos, sin, mla_w_in[j], mla_g_q[j], mla_w_uq[j],
                                     mla_g_kv[j], mla_w_ukv[j], mla_w_out[j], not last)
        else:
            y_ctx, y_lat = swa_mixer(a_ctx, a_lat, cos, sin, swa_w_in[j], swa_sinks[j], swa_w_out[j], not last)
        h_lat = h_lat + ml[2] * rmsnorm(y_lat, g_norm[i, 1])
        f_lat = swiglu(adaln_in(h_lat, g_norm[i, 2], ml[3], ml[4]), w_ffn_in[i], w_ffn_out[i])
        h_lat = h_lat + ml[5] * rmsnorm(f_lat, g_norm[i, 3])
        if not last:
            h_ctx = h_ctx + mc[2] * rmsnorm(y_ctx, g_norm[i, 1])
            f_ctx = swiglu(adaln_in(h_ctx, g_norm[i, 2], mc[3], mc[4]), w_ffn_in[i], w_ffn_out[i])
            h_ctx = h_ctx + mc[5] * rmsnorm(f_ctx, g_norm[i, 3])
    return h_lat
```

```python
import math
import os
from contextlib import ExitStack

import numpy as np
import ml_dtypes
import concourse.bass as bass
import concourse.mybir as mybir
from concourse.bass_utils import run_bass_kernel_spmd

F32 = mybir.dt.float32
BF16 = mybir.dt.bfloat16
ALU = mybir.AluOpType
AF = mybir.ActivationFunctionType
AX = mybir.AxisListType

D = 2048
T_CTX = 256
T_LAT = 4096
T = T_CTX + T_LAT
NT = T // 128
NTC = T_CTX // 128
KC = D // 128
DFF = 5632
FC = DFF // 128
EPS = 1e-6
DEPTH = 4
SQD = math.sqrt(D)

SAME_ENGINE_SYNC = True
COMPUTE = ("tensor", "vector", "scalar", "gpsimd")
ENGINES = COMPUTE + ("sync",)
BARRIER_ENGINES = ("tensor", "vector", "scalar", "sync")


class SemSlot:
    __slots__ = ("sem", "cnt", "bg")

    def __init__(self, sem):
        self.sem = sem
        self.cnt = 0
        self.bg = False


class Obj:
    __slots__ = ("lw", "rd", "rdd", "slot", "name")

    def __init__(self, name=""):
        self.lw = -1
        self.rd = {}
        self.rdd = []
        self.slot = None
        self.name = name


class Tile:
    def __init__(self, t, o):
        self.t = t
        self.o = o

    def __getitem__(self, k):
        return self.t[k]


class Op:
    __slots__ = ("eng", "fn", "cdeps", "ddeps", "slot", "dcount", "needs", "ordv")


class Prog:
    def __init__(self):
        self.nc = bass.Bass("TRN2", target_bir_lowering=False)
        self.ops = []
        self.last = {}
        self.root = ExitStack()
        self.free_slots = []
        self.all_slots = []
        self.phase_objs = []
        self.phase = None
        self.n_sem = 0
        self.esems = {e: self.root.enter_context(self.nc.semaphore(f"e_{e}")) for e in ENGINES}

    def begin_phase(self):
        assert self.phase is None
        self.phase = ExitStack()
        self.phase_objs = []

    def end_phase(self):
        self.barrier()
        for o in self.phase_objs:
            if o.slot is not None:
                self.free_slots.append(o.slot)
                o.slot = None
        self.phase.close()
        self.phase = None

    def sb(self, name, shape, dtype, nobj=1):
        self.uid = getattr(self, "uid", 0) + 1
        name = f"s{self.uid}_{name}"
        t = self.phase.enter_context(self.nc.sbuf_tensor(name, list(shape), dtype))
        if nobj == 1:
            o = Obj(name)
            self.phase_objs.append(o)
            return Tile(t, o)
        objs = [Obj(f"{name}{i}") for i in range(nobj)]
        self.phase_objs.extend(objs)
        return Tile(t, objs)

    def ps(self, name, shape, dtype=F32):
        self.uid = getattr(self, "uid", 0) + 1
        name = f"p{self.uid}_{name}"
        nb = int(np.prod(shape[1:])) * (2 if dtype == BF16 else 4)
        assert nb % 2048 == 0, (name, shape)
        t = self.phase.enter_context(self.nc.psum_tensor(name, list(shape), dtype))
        o = Obj(name)
        self.phase_objs.append(o)
        return Tile(t, o)

    def _slot(self, o, fresh=False):
        if o.slot is None:
            if self.free_slots and not fresh:
                o.slot = self.free_slots.pop()
            else:
                self.n_sem += 1
                sem = self.root.enter_context(self.nc.semaphore(f"d{self.n_sem}"))
                o.slot = SemSlot(sem)
                self.all_slots.append(o.slot)
        return o.slot

    def _rec(self, eng, fn, reads, writes, slot):
        idx = len(self.ops)
        op = Op()
        op.eng = eng
        op.fn = fn
        op.slot = slot
        op.needs = False
        op.ordv = 0
        cd = {}
        dd = {}

        def dep(j):
            if j < 0:
                return
            pj = self.ops[j]
            if pj.slot is not None:
                if dd.get(pj.slot, 0) < pj.dcount:
                    dd[pj.slot] = pj.dcount
            else:
                if cd.get(pj.eng, -1) < j:
                    cd[pj.eng] = j

        for o in reads:
            dep(o.lw)
        for o in writes:
            dep(o.lw)
            for j in o.rd.values():
                dep(j)
            for j in o.rdd:
                dep(j)
        for P in list(cd.keys()):
            if P == eng and (P == "tensor" or not SAME_ENGINE_SYNC):
                del cd[P]
            else:
                self.ops[cd[P]].needs = True
        op.cdeps = cd
        op.ddeps = dd
        if slot is not None:
            slot.cnt += 16
            op.dcount = slot.cnt
        else:
            op.dcount = 0
        for o in reads:
            if slot is not None:
                o.rdd.append(idx)
            else:
                o.rd[eng] = idx
        for o in writes:
            o.lw = idx
            o.rd = {}
            o.rdd = []
        self.ops.append(op)
        self.last[eng] = idx
        return idx

    def op(self, eng, fn, reads=(), writes=()):
        return self._rec(eng, fn, reads, writes, None)

    def dma(self, eng, out, in_, reads, writes, semobj, bg=False, **kw):
        nc = self.nc
        slot = self._slot(semobj, fresh=bg)
        if bg:
            slot.bg = True
        e = getattr(nc, eng)
        return self._rec(eng, lambda: e.dma_start(out=out, in_=in_, **kw), reads, writes, slot)

    def barrier(self):
        lasts = {e: j for e, j in self.last.items() if e in BARRIER_ENGINES}
        for eng in BARRIER_ENGINES:
            op = Op()
            op.eng = eng
            op.fn = None
            op.slot = None
            op.needs = False
            op.ordv = 0
            op.dcount = 0
            cd = {}
            for P, j in lasts.items():
                pj = self.ops[j]
                if P == eng and P == "tensor":
                    continue
                if pj.slot is None and pj.fn is not None:
                    cd[P] = j
                    pj.needs = True
                else:
                    k = j
                    while k >= 0 and not (self.ops[k].eng == P and self.ops[k].slot is None
                                           and self.ops[k].fn is not None):
                        k -= 1
                    if k >= 0:
                        cd[P] = k
                        self.ops[k].needs = True
            op.cdeps = cd
            op.ddeps = {s: s.cnt for s in self.all_slots if s.cnt > 0 and not s.bg}
            self.ops.append(op)

    def finish(self):
        nc = self.nc
        cnt = {e: 0 for e in ENGINES}
        for op in self.ops:
            if op.slot is None and op.needs:
                cnt[op.eng] += 1
                op.ordv = cnt[op.eng]
        sems = self.esems
        seen = {e: {} for e in ENGINES}
        nw = 0
        for op in self.ops:
            E = getattr(nc, op.eng)
            sn = seen[op.eng]
            for P, j in op.cdeps.items():
                v = self.ops[j].ordv
                if sn.get(P, 0) < v:
                    E.wait_ge(sems[P], v)
                    sn[P] = v
                    nw += 1
            for s, c in op.ddeps.items():
                if sn.get(s, 0) < c:
                    E.wait_ge(s.sem, c)
                    sn[s] = c
                    nw += 1
            if op.fn is not None:
                ins = op.fn()
                if op.slot is not None:
                    ins.then_inc(op.slot.sem, 16)
                elif op.needs:
                    ins.then_inc(sems[op.eng], 1)
        self.stats = dict(n_ops=len(self.ops), n_waits=nw, counts=cnt, n_dma_sems=self.n_sem)
        return nc


STORE_ENG = os.environ.get("STORE_ENG", "gpsimd")
GLA_H, GLA_DK, GLA_DV = 4, 256, 512
MLA_H = 16
MLA_SCALE = (128 + 64) ** -0.5
SWA_SCALE = 64 ** -0.5


def _consts():
    c = {}
    c["ident"] = np.eye(128, dtype=np.float32).astype(ml_dtypes.bfloat16)
    c["identf"] = np.eye(128, dtype=np.float32)
    t = np.arange(T_LAT)
    row = (t // 64).astype(np.float32)
    col = (t % 64).astype(np.float32)
    inv = (10000.0 ** (-np.arange(16, dtype=np.float32) / 16)).astype(np.float32)
    ang = np.concatenate([row[:, None] * inv, col[:, None] * inv], axis=-1).astype(np.float32)
    c["cs"] = np.stack([np.cos(ang), np.sin(ang)], axis=1).astype(np.float32)
    j = np.arange(128)[:, None]
    i = np.arange(128)[None, :]
    le = (j <= i).astype(np.float32)
    ge = (j >= i).astype(np.float32)
    lt = (j < i).astype(np.float32)
    gt = (j > i).astype(np.float32)
    c["masks"] = np.stack([le, ge]).astype(ml_dtypes.bfloat16)
    c["tri"] = (np.stack([le, ge, gt, lt]) * (-1.0 / 16.0)).astype(np.float32)
    c["ones_bf"] = np.ones((128, 128), dtype=np.float32).astype(ml_dtypes.bfloat16)
    c["ones_f"] = np.ones((1, 128), dtype=np.float32)
    c["ones_f128"] = np.ones((128, 128), dtype=np.float32)
    return c


class Builder:
    def __init__(self, layers=(0, 1, 2, 3), debug_outs=(), final_layer=DEPTH - 1):
        self.P = Prog()
        self.nc = self.P.nc
        self.layers = list(layers)
        self.debug_outs = set(debug_outs)
        self.final_layer = final_layer
        self.in_decl = {}
        nc = self.nc
        self.out = nc.dram_tensor("out", [T_LAT, D], F32, kind="ExternalOutput").ap()
        self.MODV = self.scr("MODV", [DEPTH, 2, 6, D], F32)
        self.H = self.scr("H", [T, D], F32)
        self.A2T = self.scr("A2T", [NT, 128, D], BF16)
        self.OGT = self.scr("OGT", [NT, 128, D], BF16)
        self.o_modv = [Obj(f"modv{i}") for i in range(DEPTH)]
        self.o_H = [Obj(f"H{t}") for t in range(NT)]
        self.o_A2T = [Obj(f"A2T{t}") for t in range(NT)]
        self.o_OGT = [Obj(f"OGT{t}") for t in range(NT)]
        self.o_out = [Obj(f"out{t}") for t in range(NT)]
        self.wb = {}
        self.nscr = 0
        self.prep_gate = []

    def inp(self, name, shape, dt=F32):
        if name not in self.in_decl:
            self.in_decl[name] = self.nc.dram_tensor(name, list(shape), dt, kind="ExternalInput").ap()
        return self.in_decl[name]

    def scr(self, name, shape, dt):
        kind = "ExternalOutput" if name in self.debug_outs else "Internal"
        return self.nc.dram_tensor(name, list(shape), dt, kind=kind).ap()

    def prep_nat(self, key, src, shape, pieces=4):
        P = self.P
        K, N = shape
        dst = self.scr(f"wb_{key}", [K, N], BF16)
        objs = []
        step = K // pieces
        so = Obj(f"wbs_{key}")
        for q in range(pieces):
            o = Obj(f"wb_{key}{q}")
            P.dma("gpsimd", dst[q * step:(q + 1) * step, :], src[q * step:(q + 1) * step, :], list(self.prep_gate), [o], so, bg=True)
            objs.append(o)
        self.wb[key] = (dst, objs)
        return dst, objs

    def prep_ffn(self, i):
        P = self.P
        w_in = self.inp("w_ffn_in", [DEPTH, D, 2 * DFF])[i]
        w_out = self.inp("w_ffn_out", [DEPTH, DFF, D])[i]
        wbin = self.scr(f"wbin{i}", [FC, 128, KC, 2, 128], BF16)
        wbout = self.scr(f"wbout{i}", [KC, 128, FC, 128], BF16)
        oin, oout = [], []
        so_in, so_out = Obj(f"wbins{i}"), Obj(f"wbouts{i}")
        for kc in range(KC):
            for g in range(2):
                o = Obj(f"wbin{i}_{kc}_{g}")
                P.dma("gpsimd", wbin[:, :, kc, g, :].rearrange("j p c -> p j c"),
                      w_in[kc * 128:(kc + 1) * 128, g * DFF:(g + 1) * DFF].rearrange("p (j c) -> p j c", c=128),
                      list(self.prep_gate), [o], so_in, bg=True)
                oin.append(o)
        for kc in range(FC):
            o = Obj(f"wbout{i}_{kc}")
            P.dma("gpsimd", wbout[:, :, kc, :].rearrange("n p c -> p n c"),
                  w_out[kc * 128:(kc + 1) * 128, :].rearrange("p (n c) -> p n c", c=128),
                  list(self.prep_gate), [o], so_out, bg=True)
            oout.append(o)
        self.wb[f"ffn{i}"] = (wbin, oin, wbout, oout)

    def load_consts(self, want):
        P = self.P
        c = {}
        if "ident" in want:
            c["ident"] = P.sb("ident", [128, 128], BF16)
            P.dma("sync", c["ident"][:], self.inp("ident", [128, 128], BF16), [], [c["ident"].o], c["ident"].o)
        if "identf" in want:
            c["identf"] = P.sb("identf", [128, 128], F32)
            P.dma("sync", c["identf"][:], self.inp("identf", [128, 128], F32), [], [c["identf"].o], c["identf"].o)
        if "cs" in want:
            c["cs"] = P.sb("cs", [128, T_LAT // 128, 2, 32], F32)
            P.dma("sync", c["cs"][:], self.inp("cs", [T_LAT, 2, 32], F32).rearrange("(n p) a f -> p n a f", p=128),
                  [], [c["cs"].o], c["cs"].o)
        if "masks" in want:
            c["masks"] = P.sb("masks", [128, 2, 128], BF16)
            P.dma("sync", c["masks"][:], self.inp("masks", [2, 128, 128], BF16).rearrange("m p i -> p m i"),
                  [], [c["masks"].o], c["masks"].o)
        if "tri" in want:
            c["tri"] = P.sb("tri", [128, 4, 128], F32)
            P.dma("sync", c["tri"][:], self.inp("tri", [4, 128, 128], F32).rearrange("m p i -> p m i"),
                  [], [c["tri"].o], c["tri"].o)
        if "ones_bf" in want:
            c["ones_bf"] = P.sb("ones_bf", [128, 128], BF16)
            P.dma("sync", c["ones_bf"][:], self.inp("ones_bf", [128, 128], BF16), [], [c["ones_bf"].o], c["ones_bf"].o)
        if "ones_f128" in want:
            c["ones_f128"] = P.sb("ones_f128", [128, 128], F32)
            P.dma("sync", c["ones_f128"][:], self.inp("ones_f128", [128, 128], F32), [], [c["ones_f128"].o], c["ones_f128"].o)
        if "ones_f" in want:
            c["ones_f"] = P.sb("ones_f", [1, 128], F32)
            P.dma("sync", c["ones_f"][:], self.inp("ones_f", [1, 128], F32), [], [c["ones_f"].o], c["ones_f"].o)
        return c

    def bc_load(self, tile, layer, r, v):
        self.P.dma("sync", tile[:], self.MODV[layer, r, v:v + 1, :].broadcast_to([128, D]),
                   [self.o_modv[layer]], [tile.o], tile.o)

    def norm_scale(self, src_ap, src_objs, width, nseg, ss4, ssum, rstd, sq, eps_scaled):
        P, nc = self.P, self.nc
        seg = width // nseg
        P.op("vector", lambda: nc.vector.memset(ss4[:, 0:nseg], 0.0), [], [ss4.o])
        for s in range(nseg):
            P.op("scalar", lambda s=s: nc.scalar.activation(
                out=sq[:, s * seg:(s + 1) * seg], in_=src_ap[:, s * seg:(s + 1) * seg], func=AF.Square,
                accum_out=ss4[:, s:s + 1]), list(src_objs) + [ss4.o], [sq.o, ss4.o])
        if nseg > 1:
            P.op("vector", lambda: nc.vector.reduce_sum(out=ssum[:, 0:1], in_=ss4[:, 0:nseg], axis=AX.X),
                 [ss4.o], [ssum.o])
            s_in = ssum
        else:
            s_in = ss4
        P.op("vector", lambda: nc.vector.tensor_scalar_add(out=rstd[:, 0:1], in0=s_in[:, 0:1], scalar1=eps_scaled),
             [s_in.o], [rstd.o])
        P.op("scalar", lambda: nc.scalar.activation(out=rstd[:, 0:1], in_=rstd[:, 0:1], func=AF.Sqrt),
             [rstd.o], [rstd.o])
        P.op("vector", lambda: nc.vector.reciprocal(out=rstd[:, 0:1], in_=rstd[:, 0:1]), [rstd.o], [rstd.o])

    def adaln(self, h, G, SH, B, aT_view, aT_objs, ident):
        P, nc = self.P, self.nc
        self.norm_scale(h[:], [h.o], D, 1, B["ss4"], B["ssum"], B["rstd"], B["sq"], D * EPS)
        tmp, abf, ptr = B["tmp"], B["abf"], B["ptr"]
        P.op("vector", lambda: nc.vector.scalar_tensor_tensor(out=tmp[:], in0=h[:], scalar=B["rstd"][:, 0:1], in1=G[:],
                                                              op0=ALU.mult, op1=ALU.mult),
             [h.o, B["rstd"].o, G.o], [tmp.o])
        P.op("vector", lambda: nc.vector.tensor_tensor(out=abf[:], in0=tmp[:], in1=SH[:], op=ALU.add),
             [tmp.o, SH.o], [abf.o])
        for kc in range(KC):
            P.op("tensor", lambda kc=kc: nc.tensor.transpose(out=ptr[:, kc * 128:(kc + 1) * 128],
                                                            in_=abf[:, kc * 128:(kc + 1) * 128], identity=ident[:]),
                 [abf.o, ident.o], [ptr.o])
        P.op("scalar", lambda: nc.scalar.activation(out=aT_view, in_=ptr[:].rearrange("p (k j) -> p k j", k=KC),
                                                    func=AF.Copy), [ptr.o], list(aT_objs))

    def norm_bufs(self):
        P = self.P
        return dict(ss4=P.sb("ss4", [128, 8], F32), ssum=P.sb("ssum", [128, 1], F32), rstd=P.sb("rstd", [128, 1], F32),
                    sq=P.sb("sq", [128, D], BF16), tmp=P.sb("tmp", [128, D], F32), abf=P.sb("abf", [128, D], BF16),
                    ptr=P.ps("ptr", [128, D], BF16))

    def rope(self, X, x_objs, cs, n, O, o_obj, H, tmps):
        P, nc = self.P, self.nc
        cos = cs[:, n, 0, :].unsqueeze(1).broadcast_to([128, H, 32])
        sin = cs[:, n, 1, :].unsqueeze(1).broadcast_to([128, H, 32])
        t1, t2 = tmps
        a = t1[:, 0:H * 32].rearrange("p (h f) -> p h f", h=H)
        b = t2[:, 0:H * 32].rearrange("p (h f) -> p h f", h=H)
        V = nc.vector
        rd = list(x_objs) + [cs.o]
        P.op("vector", lambda: V.tensor_tensor(out=a, in0=X[:, :, 0, :], in1=cos, op=ALU.mult), rd, [t1.o])
        P.op("vector", lambda: V.tensor_tensor(out=b, in0=X[:, :, 1, :], in1=sin, op=ALU.mult), rd, [t2.o])
        P.op("vector", lambda: V.tensor_tensor(out=O[:, :, 0, :], in0=a, in1=b, op=ALU.subtract), [t1.o, t2.o], [o_obj])
        P.op("vector", lambda: V.tensor_tensor(out=a, in0=X[:, :, 1, :], in1=cos, op=ALU.mult), rd + [o_obj], [t1.o])
        P.op("vector", lambda: V.tensor_tensor(out=b, in0=X[:, :, 0, :], in1=sin, op=ALU.mult), rd + [o_obj], [t2.o])
        P.op("vector", lambda: V.tensor_tensor(out=O[:, :, 1, :], in0=a, in1=b, op=ALU.add), [t1.o, t2.o], [o_obj])

    @staticmethod
    def supertiles(tiles, n=4):
        out = []
        cur = []
        for t in tiles:
            if cur and (len(cur) == n or (cur[-1] < NTC) != (t < NTC) or t != cur[-1] + 1):
                out.append(cur)
                cur = []
            cur.append(t)
        if cur:
            out.append(cur)
        return out

    def phase_init(self):
        P = self.P
        x = self.inp("x", [T_LAT, D])
        ctx = self.inp("ctx", [T_CTX, D])
        o = Obj("init")
        P.dma("sync", self.H[0:T_CTX, :], ctx, [], self.o_H[0:NTC], o)
        for q in range(4):
            a, b = q * 1024, (q + 1) * 1024
            P.dma("sync", self.H[T_CTX + a:T_CTX + b, :], x[a:b, :], [], self.o_H[NTC + q * 8:NTC + (q + 1) * 8], o)

    def phase_mod(self, mod_layers):
        P, nc = self.P, self.nc
        w_ada = self.inp("w_ada", [DEPTH, D, 6 * D])
        b_ada = self.inp("b_ada", [DEPTH, 6 * D])
        g_norm = self.inp("g_norm", [DEPTH, 4, D])
        ccd = self.inp("cc", [2, D])
        P.begin_phase()
        cc = P.sb("cc", [128, 2, KC], F32)
        sc = P.sb("sc", [128, 2, KC], F32)
        wb = [P.sb(f"wada{i}", [128, KC, 512], F32) for i in range(2)]
        bb = [P.sb(f"ba{i}", [2, 512], F32) for i in range(2)]
        gn = P.sb("gn", [2, 4, D], F32)
        msb = P.sb("msb", [2, 6, D], F32)
        mps = [P.ps(f"mps{i}", [128, 512], F32) for i in range(2)]
        V = nc.vector
        P.dma("sync", cc[:], ccd.rearrange("r (kc p) -> p r kc", p=128), [], [cc.o], cc.o,
              allow_slow_non_contiguous=True)
        P.op("scalar", lambda: nc.scalar.activation(out=sc[:], in_=cc[:], func=AF.Silu), [cc.o], [sc.o])
        k = 0
        for i in mod_layers:
            P.dma("sync", gn[:].rearrange("p a d -> p (a d)"),
                  g_norm[i:i + 1].rearrange("o a d -> o (a d)").broadcast_to([2, 4 * D]), [], [gn.o], gn.o)
            for n in range(24):
                w = wb[k % 2]
                mp = mps[k % 2]
                ba = bb[k % 2]
                k += 1
                P.dma("sync", w[:], w_ada[i][:, n * 512:(n + 1) * 512].rearrange("(kc p) n -> p kc n", p=128),
                      [], [w.o], w.o)
                P.dma("sync", ba[:], b_ada[i:i + 1, n * 512:(n + 1) * 512].broadcast_to([2, 512]), [], [ba.o], ba.o)
                for kc in range(KC):
                    P.op("tensor", lambda mp=mp, w=w, kc=kc: nc.tensor.matmul(
                        mp[0:2, :], lhsT=sc[:, :, kc], rhs=w[:, kc, :], start=(kc == 0), stop=(kc == KC - 1)),
                        [sc.o, w.o], [mp.o])
                j, off = divmod(n * 512, D)
                P.op("vector", lambda mp=mp, j=j, off=off, ba=ba: V.tensor_tensor(
                    out=msb[:, j, off:off + 512], in0=mp[0:2, :], in1=ba[:], op=ALU.add), [mp.o, ba.o], [msb.o])
            for (v, gi, addone) in ((1, 0, True), (2, 1, False), (4, 2, True), (5, 3, False)):
                if addone:
                    P.op("vector", lambda v=v, gi=gi: V.scalar_tensor_tensor(
                        out=msb[:, v, :], in0=msb[:, v, :], scalar=1.0, in1=gn[:, gi, :], op0=ALU.add, op1=ALU.mult),
                        [msb.o, gn.o], [msb.o])
                else:
                    P.op("vector", lambda v=v, gi=gi: V.tensor_tensor(
                        out=msb[:, v, :], in0=msb[:, v, :], in1=gn[:, gi, :], op=ALU.mult), [msb.o, gn.o], [msb.o])
                P.op("vector", lambda v=v: V.tensor_scalar_mul(out=msb[:, v, :], in0=msb[:, v, :], scalar1=SQD),
                     [msb.o], [msb.o])
            P.dma("sync", self.MODV[i].rearrange("r v d -> r (v d)"), msb[:].rearrange("p v d -> p (v d)"),
                  [msb.o], [self.o_modv[i]], msb.o)
        P.end_phase()

    def pa_loop(self, i, tiles, Wb, Wobjs, blocks, consume, per_st_begin=None, per_st_end=None, extra_ps=2, C=None):
        P, nc = self.P, self.nc
        B = self.norm_bufs()
        ident = C["ident"]
        G1 = P.sb("G1", [128, D], F32)
        SH1 = P.sb("SH1", [128, D], F32)
        hb = [P.sb(f"h{q}", [128, D], F32) for q in range(2)]
        aTs = [P.sb(f"aT{q}", [128, KC, 512], BF16) for q in range(2)]
        wblk = [P.sb(f"wblk{q}", [128, KC, 512], BF16) for q in range(2)]
        pps = [P.ps(f"pp{q}", [128, 512], F32) for q in range(extra_ps)]
        Wv = Wb.rearrange("(kc p) n -> p kc n", p=128)
        cur_r = None
        kk = 0
        pk = 0
        for sti_, st in enumerate(self.supertiles(tiles)):
            aT = aTs[sti_ % 2]
            r = 1 if st[0] < NTC else 0
            if r != cur_r:
                self.bc_load(G1, i, r, 1)
                self.bc_load(SH1, i, r, 0)
                cur_r = r
            for tt, t in enumerate(st):
                h = hb[kk % 2]
                kk += 1
                P.dma("sync", h[:], self.H[t * 128:(t + 1) * 128, :], [self.o_H[t]], [h.o], h.o)
                self.adaln(h, G1, SH1, B, aT[:, :, tt * 128:(tt + 1) * 128], [aT.o], ident)
            if per_st_begin:
                per_st_begin(st, aT)
            import os
            DBG = int(os.environ.get("KDBG", "9"))
            if DBG < 2:
                continue
            for bi, (c0, wd) in enumerate(blocks):
                w = wblk[bi % 2]
                P.dma("sync", w[:, :, 0:wd], Wv[:, :, c0:c0 + wd], list(Wobjs), [w.o], w.o)
                for tt, t in enumerate(st):
                    pp = pps[pk % extra_ps]
                    pk += 1
                    for kc in range(KC):
                        P.op("tensor", lambda pp=pp, w=w, kc=kc, tt=tt, wd=wd, aT=aT: nc.tensor.matmul(
                            pp[:, 0:wd], lhsT=aT[:, kc, tt * 128:(tt + 1) * 128], rhs=w[:, kc, 0:wd],
                            start=(kc == 0), stop=(kc == KC - 1)), [aT.o, w.o], [pp.o])
                    if DBG >= 3:
                        consume(st, tt, t, bi, pp)
            if per_st_end:
                per_st_end(st, aT)
        return B, pps

    def phase_post(self, i, Wb, Wobjs, tiles):
        P, nc = self.P, self.nc
        P.begin_phase()
        C = self.load_consts(["ident"])
        ident = C["ident"]
        Bs = [self.norm_bufs() for _ in range(2)]
        B1s = [dict(ss4=P.sb("ss4b", [128, 8], F32), ssum=P.sb("ssumb", [128, 1], F32), rstd=P.sb("rstdb", [128, 1], F32),
                    sq=Bs[q]["sq"]) for q in range(2)]
        wout = P.sb("wout", [128, KC, D], BF16)
        Wv = Wb.rearrange("(kc p) n -> p kc n", p=128)
        for q in range(4):
            P.dma("sync", wout[:, q * 4:(q + 1) * 4, :], Wv[:, q * 4:(q + 1) * 4, :], list(Wobjs), [wout.o], wout.o)
        GG1 = P.sb("GG1", [128, D], F32)
        G2 = P.sb("G2", [128, D], F32)
        SH2 = P.sb("SH2", [128, D], F32)
        ogb = [P.sb(f"og{q}", [128, KC, 128], BF16) for q in range(2)]
        hb = [P.sb(f"h{q}", [128, D], F32) for q in range(2)]
        t5 = [P.sb(f"t5{q}", [128, 512], F32) for q in range(2)]
        a2b = [P.sb(f"a2T{q}", [128, KC, 128], BF16) for q in range(2)]
        y = P.ps("y", [128, D], F32)
        V = nc.vector
        cur_r = None
        for k, t in enumerate(tiles):
            r = 1 if t < NTC else 0
            if r != cur_r:
                self.bc_load(GG1, i, r, 2)
                self.bc_load(G2, i, r, 4)
                self.bc_load(SH2, i, r, 3)
                cur_r = r
            og, h, a2 = ogb[k % 2], hb[k % 2], a2b[k % 2]
            B = Bs[k % 2]
            B1 = B1s[k % 2]
            P.dma("sync", og[:].rearrange("p k j -> p (k j)"), self.OGT[t], [self.o_OGT[t]], [og.o], og.o)
            P.dma("sync", h[:], self.H[t * 128:(t + 1) * 128, :], [self.o_H[t]], [h.o], h.o)
            for nb in range(4):
                for kc in range(KC):
                    P.op("tensor", lambda nb=nb, kc=kc, og=og: nc.tensor.matmul(
                        y[:, nb * 512:(nb + 1) * 512], lhsT=og[:, kc, :], rhs=wout[:, kc, nb * 512:(nb + 1) * 512],
                        start=(kc == 0), stop=(kc == KC - 1)), [og.o, wout.o], [y.o])
            self.norm_scale(y[:], [y.o], D, 4, B1["ss4"], B1["ssum"], B1["rstd"], B1["sq"], D * EPS)
            for nb in range(4):
                tq = t5[nb % 2]
                sl = slice(nb * 512, (nb + 1) * 512)
                P.op("vector", lambda tq=tq, sl=sl, B1=B1: V.scalar_tensor_tensor(
                    out=tq[:], in0=y[:, sl], scalar=B1["rstd"][:, 0:1], in1=GG1[:, sl], op0=ALU.mult, op1=ALU.mult),
                    [y.o, B1["rstd"].o, GG1.o], [tq.o])
                P.op("vector", lambda tq=tq, sl=sl, h=h: V.tensor_tensor(out=h[:, sl], in0=h[:, sl], in1=tq[:], op=ALU.add),
                     [tq.o, h.o], [h.o])
            P.dma("scalar", self.H[t * 128:(t + 1) * 128, :], h[:], [h.o], [self.o_H[t]], h.o)
            self.adaln(h, G2, SH2, B, a2[:], [a2.o], ident)
            P.dma("scalar", self.A2T[t], a2[:].rearrange("p k j -> p (k j)"), [a2.o], [self.o_A2T[t]], a2.o)
        P.end_phase()

    def phase_ffn(self, i, tiles, to_out):
        P, nc = self.P, self.nc
        wbin, oin, wbout, oout = self.wb[f"ffn{i}"]
        P.begin_phase()
        C = self.load_consts(["identf"])
        identf = C["identf"]
        a2 = P.sb("a2st", [128, KC, 512], BF16)
        hid = P.sb("hid", [128, FC, 512], BF16, nobj=FC)
        wi = [P.sb(f"wi{q}", [128, KC, 2, 128], BF16) for q in range(4)]
        wo = [P.sb(f"wo{q}", [128, FC, 128], BF16) for q in range(3)]
        yT = P.sb("yT", [128, KC, 512], F32, nobj=KC)
        sg = [P.sb(f"sg{q}", [128, 512], F32) for q in range(2)]
        hb = [P.sb(f"h{q}", [128, D], F32) for q in range(2)]
        t5 = [P.sb(f"t5{q}", [128, 512], F32) for q in range(2)]
        GG2 = P.sb("GG2", [128, D], F32)
        ss4 = P.sb("ss4", [128, 8], F32)
        ssum = P.sb("ssum", [128, 1], F32)
        rstd = P.sb("rstd", [128, 1], F32)
        sq = P.sb("sq", [128, D], BF16)
        pb = [P.ps(f"pb{q}", [128, 512], F32) for q in range(4)]
        ytm = P.ps("ytm", [128, D], F32)
        V = nc.vector
        cur_r = None
        kk = 0
        for st in self.supertiles(tiles):
            n = len(st) * 128
            r = 1 if st[0] < NTC else 0
            if r != cur_r:
                self.bc_load(GG2, i, r, 5)
                cur_r = r
            for tt, t in enumerate(st):
                P.dma("sync", a2[:, :, tt * 128:(tt + 1) * 128], self.A2T[t].rearrange("p (k j) -> p k j", k=KC),
                      [self.o_A2T[t]], [a2.o], a2.o)
            for j in range(FC):
                w = wi[j % 4]
                P.dma("sync", w[:].rearrange("p k g c -> p (k g c)"), wbin[j].rearrange("p k g c -> p (k g c)"),
                      list(oin), [w.o], w.o)
                gp, up = pb[(j % 2) * 2], pb[(j % 2) * 2 + 1]
                for g_, pp in ((0, gp), (1, up)):
                    for kc in range(KC):
                        P.op("tensor", lambda g_=g_, pp=pp, kc=kc, w=w, n=n: nc.tensor.matmul(
                            pp[:, 0:n], lhsT=w[:, kc, g_, :], rhs=a2[:, kc, 0:n], start=(kc == 0), stop=(kc == KC - 1)),
                            [w.o, a2.o], [pp.o])
                s_ = sg[j % 2]
                P.op("scalar", lambda s_=s_, gp=gp, n=n: nc.scalar.activation(out=s_[:, 0:n], in_=gp[:, 0:n], func=AF.Silu),
                     [gp.o], [s_.o])
                P.op("vector", lambda s_=s_, up=up, j=j, n=n: V.tensor_tensor(out=hid[:, j, 0:n], in0=s_[:, 0:n],
                                                                            in1=up[:, 0:n], op=ALU.mult),
                     [s_.o, up.o], [hid.o[j]])
            for c in range(KC):
                w = wo[c % 3]
                P.dma("sync", w[:].rearrange("p k c -> p (k c)"), wbout[c].rearrange("p k c -> p (k c)"),
                      list(oout), [w.o], w.o)
                pp = pb[c % 4]
                for kc in range(FC):
                    P.op("tensor", lambda pp=pp, w=w, kc=kc, n=n: nc.tensor.matmul(
                        pp[:, 0:n], lhsT=w[:, kc, :], rhs=hid[:, kc, 0:n], start=(kc == 0), stop=(kc == FC - 1)),
                        [w.o, hid.o[kc]], [pp.o])
                P.op("scalar", lambda pp=pp, c=c, n=n: nc.scalar.activation(out=yT[:, c, 0:n], in_=pp[:, 0:n], func=AF.Copy),
                     [pp.o], [yT.o[c]])
            for tt, t in enumerate(st):
                h = hb[kk % 2]
                kk += 1
                P.dma("sync", h[:], self.H[t * 128:(t + 1) * 128, :], [self.o_H[t]], [h.o], h.o)
                for c in range(KC):
                    P.op("tensor", lambda c=c, tt=tt: nc.tensor.transpose(
                        out=ytm[:, c * 128:(c + 1) * 128], in_=yT[:, c, tt * 128:(tt + 1) * 128], identity=identf[:]),
                        [yT.o[c], identf.o], [ytm.o])
                Bn = dict(ss4=ss4, ssum=ssum, rstd=rstd, sq=sq)
                self.norm_scale(ytm[:], [ytm.o], D, 4, ss4, ssum, rstd, sq, D * EPS)
                for nb in range(4):
                    tq = t5[nb % 2]
                    sl = slice(nb * 512, (nb + 1) * 512)
                    P.op("vector", lambda tq=tq, sl=sl: V.scalar_tensor_tensor(
                        out=tq[:], in0=ytm[:, sl], scalar=rstd[:, 0:1], in1=GG2[:, sl], op0=ALU.mult, op1=ALU.mult),
                        [ytm.o, rstd.o, GG2.o], [tq.o])
                    P.op("vector", lambda tq=tq, sl=sl, h=h: V.tensor_tensor(out=h[:, sl], in0=h[:, sl], in1=tq[:], op=ALU.add),
                         [tq.o, h.o], [h.o])
                if to_out and t >= NTC:
                    n_ = t - NTC
                    P.dma("sync", self.out[n_ * 128:(n_ + 1) * 128, :], h[:], [h.o], [self.o_out[t]], h.o)
                else:
                    P.dma("sync", self.H[t * 128:(t + 1) * 128, :], h[:], [h.o], [self.o_H[t]], h.o)
        P.end_phase()

    def phase_swa(self, i, j, tiles):
        P, nc = self.P, self.nc
        sinks = self.inp("swa_sinks", [1, 32])
        Wb, Wo = self.wb["swa_in"]
        if not hasattr(self, "QT"):
            self.QT = self.scr("QT", [4, NT, 64, 1024], BF16)
            self.o_QT = [Obj(f"QT{t}") for t in range(NT)]
        QT, o_QT = self.QT, self.o_QT
        P.begin_phase()
        C = self.load_consts(["ident", "cs", "masks"])
        ident, cs, masks = C["ident"], C["cs"], C["masks"]
        V = nc.vector
        KT = P.sb("KT", [64, 4, T], BF16)
        VA = P.sb("VA", [128, NT, 4, 72], BF16)
        P.op("vector", lambda: V.memset(VA[:], 1.0), [], [VA.o])
        es = P.sb("es", [128, 32], F32)
        P.dma("sync", es[:], sinks[0:1, :].broadcast_to([128, 32]), [], [es.o], es.o)
        P.op("scalar", lambda: nc.scalar.activation(out=es[:], in_=es[:], func=AF.Exp), [es.o], [es.o])
        qr = [P.sb(f"qr{q}", [128, 512], BF16) for q in range(2)]
        kr = P.sb("kr", [128, 256], BF16)
        tmps = (P.sb("rt1", [128, 256], F32), P.sb("rt2", [128, 256], F32))
        ptq = P.ps("ptq", [64, 1024], BF16)
        ptk = P.ps("ptk", [64, 1024], BF16)
        qTs = [P.sb(f"qTs{q}", [64, 1024], BF16) for q in range(2)]
        cnt = [0]

        import os
        DBG = int(os.environ.get("KDBG", "9"))

        def consume(st, tt, t, bi, pp):
            lat = t >= NTC
            n = t - NTC
            if DBG == 3 and bi >= 4:
                return
            if DBG in (4, 6, 7, 8) and bi < 4:
                return
            if DBG == 5:
                lat = False
            if bi < 4:
                q_ = qr[cnt[0] % 2]
                qs = qTs[cnt[0] % 2]
                cnt[0] += 1
                if lat:
                    X = pp[:, 0:512].rearrange("p (h a f) -> p h a f", h=8, a=2)
                    O = q_[:, 0:512].rearrange("p (h a f) -> p h a f", h=8, a=2)
                    self.rope(X, [pp.o], cs, n, O, q_.o, 8, tmps)
                else:
                    P.op("scalar", lambda: nc.scalar.activation(out=q_[:], in_=pp[:, 0:512], func=AF.Copy), [pp.o], [q_.o])
                for hh in range(8):
                    P.op("tensor", lambda hh=hh: nc.tensor.transpose(out=ptq[:, hh * 128:(hh + 1) * 128],
                                                                    in_=q_[:, hh * 64:(hh + 1) * 64], identity=ident[:]),
                         [q_.o, ident.o], [ptq.o])
                P.op("scalar", lambda: nc.scalar.activation(out=qs[:], in_=ptq[:], func=AF.Copy), [ptq.o], [qs.o])
                P.dma("scalar", QT[bi, t], qs[:], [qs.o], [o_QT[t]], qs.o)
            else:
                if lat:
                    X = pp[:, 0:256].rearrange("p (h a f) -> p h a f", h=4, a=2)
                    O = kr[:, 0:256].rearrange("p (h a f) -> p h a f", h=4, a=2)
                    self.rope(X, [pp.o], cs, n, O, kr.o, 4, tmps)
                else:
                    P.op("scalar", lambda: nc.scalar.activation(out=kr[:], in_=pp[:, 0:256], func=AF.Copy), [pp.o], [kr.o])
                if DBG == 6:
                    return
                for g in range(4):
                    P.op("tensor", lambda g=g: nc.tensor.transpose(out=ptk[:, g * 128:(g + 1) * 128],
                                                                  in_=kr[:, g * 64:(g + 1) * 64], identity=ident[:]),
                         [kr.o, ident.o], [ptk.o])
                if DBG == 7:
                    return
                P.op("scalar", lambda: nc.scalar.activation(out=KT[:, :, t * 128:(t + 1) * 128],
                                                            in_=ptk[:, 0:512].rearrange("p (g j) -> p g j", g=4), func=AF.Copy),
                     [ptk.o], [KT.o])
                if DBG == 8:
                    return
                P.op("scalar", lambda: nc.scalar.activation(out=VA[:, t, :, 0:64],
                                                            in_=pp[:, 256:512].rearrange("p (g d) -> p g d", g=4),
                                                            func=AF.Copy), [pp.o], [VA.o])

        blocks = [(0, 512), (512, 512), (1024, 512), (1536, 512), (2048, 512)]
        B, pps = self.pa_loop(i, tiles, Wb, Wo, blocks, consume, C=C)
        if self.stop == "pa":
            P.end_phase()
            return
        PT = [P.sb(f"PT{q}", [128, 5, 1024], BF16) for q in range(1)]
        ogt = [P.sb(f"ogt{q}", [128, D], BF16) for q in range(1)]
        ogT = [P.sb(f"ogT{q}", [128, KC, 128], BF16) for q in range(1)]
        den = P.sb("den", [128, 8], F32)
        rden = P.sb("rden", [128, 8], F32)
        ops_ = [P.ps(f"ops{q}", [128, 4, 128], F32) for q in range(2)]
        ptr = B["ptr"]
        gk = 0
        for k, t in enumerate(tiles):
            lat = t >= NTC
            n = t - NTC
            kts = [(0, None), (1, None)]
            if lat:
                if n > 0:
                    kts.append((t - 1, 1))
                kts.append((t, None))
                if n < T_LAT // 128 - 1:
                    kts.append((t + 1, 0))
            og = ogt[0]
            for g in range(4):
                qs = qTs[gk % 2]
                pt = PT[0]
                gk += 1
                P.dma("sync", qs[:], QT[g, t], [o_QT[t]], [qs.o], qs.o)
                for ki, (kt, m) in enumerate(kts):
                    for half in range(2):
                        pp = pps[half]
                        P.op("tensor", lambda pp=pp, kt=kt, half=half, g=g, qs=qs: nc.tensor.matmul(
                            pp[:, 0:512], lhsT=KT[:, g, kt * 128:(kt + 1) * 128], rhs=qs[:, half * 512:(half + 1) * 512],
                            start=True, stop=True), [KT.o, qs.o], [pp.o])
                        P.op("scalar", lambda pp=pp, pt=pt, ki=ki, half=half: nc.scalar.activation(
                            out=pt[:, ki, half * 512:(half + 1) * 512], in_=pp[:, 0:512], func=AF.Exp, scale=SWA_SCALE),
                            [pp.o], [pt.o])
                    if m is not None:
                        P.op("vector", lambda pt=pt, ki=ki, m=m: V.tensor_tensor(
                            out=pt[:, ki, :].rearrange("p (h q) -> p h q", h=8),
                            in0=pt[:, ki, :].rearrange("p (h q) -> p h q", h=8),
                            in1=masks[:, m, :].unsqueeze(1).broadcast_to([128, 8, 128]), op=ALU.mult),
                            [pt.o, masks.o], [pt.o])
                for hh in range(8):
                    ob = ops_[hh // 4]
                    for ki, (kt, m) in enumerate(kts):
                        P.op("tensor", lambda ob=ob, hh=hh, ki=ki, kt=kt, pt=pt, g=g, nk=len(kts): nc.tensor.matmul(
                            ob[:, hh % 4, 0:65], lhsT=pt[:, ki, hh * 128:(hh + 1) * 128], rhs=VA[:, kt, g, 0:65],
                            start=(ki == 0), stop=(ki == nk - 1)), [pt.o, VA.o], [ob.o])
                for half in range(2):
                    ob = ops_[half]
                    P.op("vector", lambda ob=ob, half=half, g=g: V.tensor_tensor(
                        out=den[:, half * 4:(half + 1) * 4], in0=ob[:, :, 64],
                        in1=es[:, g * 8 + half * 4:g * 8 + half * 4 + 4], op=ALU.add), [ob.o, es.o], [den.o])
                P.op("vector", lambda: V.reciprocal(out=rden[:], in_=den[:]), [den.o], [rden.o])
                for half in range(2):
                    ob = ops_[half]
                    c0 = g * 512 + half * 256
                    P.op("vector", lambda ob=ob, half=half, c0=c0, og=og: V.tensor_tensor(
                        out=og[:, c0:c0 + 256].rearrange("p (h d) -> p h d", h=4), in0=ob[:, :, 0:64],
                        in1=rden[:, half * 4:(half + 1) * 4].unsqueeze(2).broadcast_to([128, 4, 64]), op=ALU.mult),
                        [ob.o, rden.o], [og.o])
            oT = ogT[0]
            for kc in range(KC):
                P.op("tensor", lambda kc=kc, og=og: nc.tensor.transpose(out=ptr[:, kc * 128:(kc + 1) * 128],
                                                                       in_=og[:, kc * 128:(kc + 1) * 128], identity=ident[:]),
                     [og.o, ident.o], [ptr.o])
            P.op("scalar", lambda oT=oT: nc.scalar.activation(out=oT[:], in_=ptr[:].rearrange("p (k j) -> p k j", k=KC),
                                                              func=AF.Copy), [ptr.o], [oT.o])
            P.dma("scalar", self.OGT[t], oT[:].rearrange("p k j -> p (k j)"), [oT.o], [self.o_OGT[t]], oT.o)
        P.end_phase()

    def prep_mla(self, j):
        P = self.P
        self.prep_nat("mla_in", self.inp("mla_w_in", [1, D, 1088])[j], [D, 1088])
        self.prep_nat("mla_out", self.inp("mla_w_out", [1, D, D])[j], [D, D])
        wuq = self.inp("mla_w_uq", [1, 512, 3072])[j]
        wukv = self.inp("mla_w_ukv", [1, 512, 4096])[j]
        d1 = self.scr("wb_uq", [512, 3072], BF16)
        d2 = self.scr("wb_ukv", [512, 4096], BF16)
        so = Obj("wbs_mla")
        objs = []
        s1 = wuq.rearrange("k (h e) -> k h e", e=192)
        s2 = wukv.rearrange("k (h e) -> k h e", e=256)
        for q in range(4):
            r = slice(q * 128, (q + 1) * 128)
            for (dd, c0, c1, dw, ss, e0, e1) in ((d1, 0, 2048, 128, s1, 0, 128), (d1, 2048, 3072, 64, s1, 128, 192),
                                                 (d2, 0, 2048, 128, s2, 0, 128), (d2, 2048, 4096, 128, s2, 128, 256)):
                o = Obj("wb_mla")
                P.dma("gpsimd", dd[r, c0:c1].rearrange("k (h d) -> k h d", d=dw), ss[r, :, e0:e1], list(self.prep_gate), [o], so, bg=True)
                objs.append(o)
        self.wb["mla_u"] = (d1, d2, objs)

    def phase_mla(self, i, j, tiles):
        P, nc = self.P, self.nc
        V = nc.vector
        Wb, Wo = self.wb["mla_in"]
        d1, d2, uobjs = self.wb["mla_u"]
        g_q = self.inp("mla_g_q", [1, 512])
        g_kv = self.inp("mla_g_kv", [1, 512])
        QNT = self.scr("QNT", [16, 128, T], BF16)
        KNT = self.scr("KNT", [16, 128, T], BF16)
        QRT = self.scr("QRT", [8, 128, T], BF16)
        KRTd = self.scr("KRTd", [128, T], BF16)
        Vm = self.scr("Vm", [NT, 128, D], BF16)
        sts = self.supertiles(tiles)
        o_QNT = [[Obj() for _ in sts] for _ in range(16)]
        o_KNT = [[Obj() for _ in sts] for _ in range(16)]
        o_QRT = [Obj() for _ in range(NT)]
        o_KRT = [Obj() for _ in range(NT)]
        o_Vm = [Obj() for _ in range(NT)]
        P.begin_phase()
        C = self.load_consts(["ident", "cs"])
        ident, cs = C["ident"], C["cs"]
        Wuq = P.sb("Wuq", [128, 4, 3072], BF16)
        Wukv = P.sb("Wukv", [128, 4, 4096], BF16)
        P.dma("sync", Wuq[:], d1.rearrange("(kc p) n -> p kc n", p=128), list(uobjs), [Wuq.o], Wuq.o)
        P.dma("sync", Wukv[:], d2.rearrange("(kc p) n -> p kc n", p=128), list(uobjs), [Wukv.o], Wukv.o)
        gbc = [P.sb(f"gbc{q}", [128, 512], F32) for q in range(2)]
        for q, g in enumerate((g_q, g_kv)):
            P.dma("sync", gbc[q][:], g[j:j + 1, :].broadcast_to([128, 512]), [], [gbc[q].o], gbc[q].o)
            P.op("vector", lambda q=q: V.tensor_scalar_mul(out=gbc[q][:], in0=gbc[q][:], scalar1=math.sqrt(512.0)),
                 [gbc[q].o], [gbc[q].o])
        cT = [P.sb(f"cT{q}", [128, 4, 512], BF16) for q in range(2)]
        cnb = P.sb("cnb", [128, 512], BF16)
        krb = P.sb("krb", [128, 128], BF16)
        krs = P.sb("krs", [128, 128], BF16)
        tmps = (P.sb("rt1", [128, 256], F32), P.sb("rt2", [128, 256], F32))
        ss4 = P.sb("mss4", [128, 8], F32)
        ssum = P.sb("mssum", [128, 1], F32)
        rstd = P.sb("mrstd", [128, 1], F32)
        sq = P.sb("msq", [128, 512], BF16)
        ptc = P.ps("ptc", [128, 1024], BF16)
        ptq = P.ps("ptq", [128, 1024], BF16)
        stg = [P.sb(f"stg{q}", [128, 512], BF16) for q in range(3)]
        qrb = P.sb("qrb", [128, 1024], BF16)
        qrs = P.sb("qrs", [128, 8, 128], BF16)
        vst = [P.sb(f"vst{q}", [128, D], BF16) for q in range(1)]
        sti_of = {}
        for si, st in enumerate(sts):
            for t in st:
                sti_of[t] = si
        ctr = [0]
        ppsref = []

        def consume(st, tt, t, bi, pp):
            lat = t >= NTC
            n = t - NTC
            if bi < 2:
                self.norm_scale(pp[:, 0:512], [pp.o], 512, 1, ss4, ssum, rstd, sq, 512 * EPS)
                P.op("vector", lambda: V.scalar_tensor_tensor(out=cnb[:], in0=pp[:, 0:512], scalar=rstd[:, 0:1],
                                                              in1=gbc[bi][:], op0=ALU.mult, op1=ALU.mult),
                     [pp.o, rstd.o, gbc[bi].o], [cnb.o])
                for kc in range(4):
                    P.op("tensor", lambda kc=kc: nc.tensor.transpose(out=ptc[:, kc * 128:(kc + 1) * 128],
                                                                    in_=cnb[:, kc * 128:(kc + 1) * 128], identity=ident[:]),
                         [cnb.o, ident.o], [ptc.o])
                P.op("scalar", lambda: nc.scalar.activation(out=cT[bi][:, :, tt * 128:(tt + 1) * 128],
                                                            in_=ptc[:, 0:512].rearrange("p (k j) -> p k j", k=4), func=AF.Copy),
                     [ptc.o], [cT[bi].o])
            else:
                if lat:
                    X = pp[:, 0:64].rearrange("p (h a f) -> p h a f", h=1, a=2)
                    O = krb[:, 0:64].rearrange("p (h a f) -> p h a f", h=1, a=2)
                    self.rope(X, [pp.o], cs, n, O, krb.o, 1, tmps)
                else:
                    P.op("scalar", lambda: nc.scalar.activation(out=krb[:, 0:64], in_=pp[:, 0:64], func=AF.Copy), [pp.o], [krb.o])
                P.op("scalar", lambda: nc.scalar.activation(out=krb[:, 64:128], in_=krb[:, 0:64], func=AF.Copy), [krb.o], [krb.o])
                P.op("tensor", lambda: nc.tensor.transpose(out=ptc[:, 512:640], in_=krb[:], identity=ident[:]),
                     [krb.o, ident.o], [ptc.o])
                P.op("scalar", lambda: nc.scalar.activation(out=krs[:], in_=ptc[:, 512:640], func=AF.Copy), [ptc.o], [krs.o])
                P.dma("scalar", KRTd[:, t * 128:(t + 1) * 128], krs[:], [krs.o], [o_KRT[t]], krs.o)

        def st_end(st, aT):
            pps = ppsref[0]
            n = len(st) * 128
            t0 = st[0]
            si = sti_of[t0]
            for h in range(16):
                for which, (Wt, cq, dst, ob) in enumerate(((Wuq, cT[0], QNT, o_QNT), (Wukv, cT[1], KNT, o_KNT))):
                    pp = pps[ctr[0] % 2]
                    sg_ = stg[ctr[0] % 3]
                    ctr[0] += 1
                    for kc in range(4):
                        P.op("tensor", lambda pp=pp, Wt=Wt, cq=cq, kc=kc, h=h: nc.tensor.matmul(
                            pp[:, 0:n], lhsT=Wt[:, kc, h * 128:(h + 1) * 128], rhs=cq[:, kc, 0:n],
                            start=(kc == 0), stop=(kc == 3)), [Wt.o, cq.o], [pp.o])
                    P.op("scalar", lambda pp=pp, sg_=sg_: nc.scalar.activation(out=sg_[:, 0:n], in_=pp[:, 0:n], func=AF.Copy),
                         [pp.o], [sg_.o])
                    P.dma("scalar", dst[h, :, t0 * 128:t0 * 128 + n], sg_[:, 0:n], [sg_.o], [ob[h][si]], sg_.o)
            for tt, t in enumerate(st):
                lat = t >= NTC
                nl = t - NTC
                for half in range(2):
                    pp = pps[ctr[0] % 2]
                    ctr[0] += 1
                    for kc in range(4):
                        P.op("tensor", lambda pp=pp, kc=kc, tt=tt, half=half: nc.tensor.matmul(
                            pp[:, 0:512], lhsT=cT[0][:, kc, tt * 128:(tt + 1) * 128],
                            rhs=Wuq[:, kc, 2048 + half * 512:2048 + (half + 1) * 512], start=(kc == 0), stop=(kc == 3)),
                            [Wuq.o, cT[0].o], [pp.o])
                    if lat:
                        X = pp[:, 0:512].rearrange("p (h a f) -> p h a f", h=8, a=2)
                        O = qrb[:, half * 512:(half + 1) * 512].rearrange("p (h a f) -> p h a f", h=8, a=2)
                        self.rope(X, [pp.o], cs, nl, O, qrb.o, 8, tmps)
                    else:
                        P.op("scalar", lambda pp=pp, half=half: nc.scalar.activation(
                            out=qrb[:, half * 512:(half + 1) * 512], in_=pp[:, 0:512], func=AF.Copy), [pp.o], [qrb.o])
                for a in range(8):
                    P.op("tensor", lambda a=a: nc.tensor.transpose(out=ptq[:, a * 128:(a + 1) * 128],
                                                                  in_=qrb[:, a * 128:(a + 1) * 128], identity=ident[:]),
                         [qrb.o, ident.o], [ptq.o])
                P.op("scalar", lambda: nc.scalar.activation(out=qrs[:], in_=ptq[:].rearrange("p (a j) -> p a j", a=8),
                                                            func=AF.Copy), [ptq.o], [qrs.o])
                P.dma("scalar", QRT[:, :, t * 128:(t + 1) * 128].rearrange("a p j -> p a j"), qrs[:], [qrs.o], [o_QRT[t]], qrs.o)
                vs = vst[0]
                for nb in range(4):
                    pp = pps[ctr[0] % 2]
                    ctr[0] += 1
                    for kc in range(4):
                        P.op("tensor", lambda pp=pp, kc=kc, tt=tt, nb=nb: nc.tensor.matmul(
                            pp[:, 0:512], lhsT=cT[1][:, kc, tt * 128:(tt + 1) * 128],
                            rhs=Wukv[:, kc, 2048 + nb * 512:2048 + (nb + 1) * 512], start=(kc == 0), stop=(kc == 3)),
                            [Wukv.o, cT[1].o], [pp.o])
                    P.op("scalar", lambda pp=pp, vs=vs, nb=nb: nc.scalar.activation(
                        out=vs[:, nb * 512:(nb + 1) * 512], in_=pp[:, 0:512], func=AF.Copy), [pp.o], [vs.o])
                P.dma("scalar", Vm[t], vs[:], [vs.o], [o_Vm[t]], vs.o)

        blocks = [(0, 512), (512, 512), (1024, 64)]

        def st_begin(st, aT):
            pass

        orig_ps = P.ps
        made = []

        def ps_hook(name, shape, dtype=F32):
            t_ = orig_ps(name, shape, dtype)
            if name.startswith("pp"):
                made.append(t_)
                if len(made) == 2:
                    ppsref.append(made)
            return t_
        P.ps = ps_hook
        self.pa_loop(i, tiles, Wb, Wo, blocks, consume, per_st_end=st_end, C=C)
        P.ps = orig_ps
        P.end_phase()
        if self.stop == "pa":
            return
        P.begin_phase()
        C = self.load_consts(["ones_f128"])
        ones = C["ones_f128"]
        accs = [P.sb(f"acc{q}", [128, 512], F32) for q in range(4)]
        KRT = P.sb("KRT", [128, T], BF16)
        P.dma("sync", KRT[:], KRTd, list(o_KRT), [KRT.o], KRT.o)
        KN = [P.sb(f"KN{q}", [128, T], BF16) for q in range(2)]
        QN = [P.sb(f"QN{q}", [128, T], BF16) for q in range(2)]
        QR = [P.sb(f"QR{q}", [128, T], BF16) for q in range(2)]
        VH = [P.sb(f"VH{q}", [128, NT, 128], BF16) for q in range(2)]
        PT = [P.sb(f"PT{q}", [128, 512], BF16) for q in range(5)]
        rden = P.sb("rden", [128, 512], F32)
        otsb = P.sb("otsb", [128, 512], F32)
        ogs = [P.sb(f"ogs{q}", [128, 4, 128], BF16) for q in range(2)]
        Sps = [P.ps(f"S{q}", [128, 512], F32) for q in range(3)]
        OTp = [P.ps(f"OT{q}", [128, 512], F32) for q in range(2)]
        DNp = [P.ps(f"DN{q}", [128, 512], F32) for q in range(2)]
        OGT4 = self.OGT.rearrange("t p (k j) -> t p k j", k=KC)
        qsts = []
        if 0 in tiles:
            qsts.append((0, 2, [0, 1]))
        for s in range(8):
            qsts.append((NTC + s * 4, 4, list(range(NT))))
        qi = 0
        pi = 0
        for h in range(16):
            b = h % 2
            pb = (h % 2) * 64
            P.dma("sync", KN[b][:], KNT[h], list(o_KNT[h]), [KN[b].o], KN[b].o)
            P.dma("sync", QN[b][:], QNT[h], list(o_QNT[h]), [QN[b].o], QN[b].o)
            if h % 2 == 0:
                QRc = QR[(h // 2) % 2]
                P.dma("sync", QRc[:], QRT[h // 2], list(o_QRT), [QRc.o], QRc.o)
            P.dma("sync", VH[b][:], Vm[:, :, h * 128:(h + 1) * 128].rearrange("t p d -> p t d"), list(o_Vm),
                  [VH[b].o], VH[b].o)
            for (t0, nt, kts) in qsts:
                OT, DN = OTp[qi % 2], DNp[qi % 2]
                og = ogs[qi % 2]
                qi += 1
                pi = self._mla_q(h, KN[b], QN[b], QRc, VH[b], KRT, ones, pb, t0, nt, kts, OT, DN, og, Sps, PT, pi,
                                 rden, otsb, OGT4, (accs[(qi % 2) * 2], accs[(qi % 2) * 2 + 1]))
        P.end_phase()

    def _mla_q(self, h, KNb, QNb, QRc, VHb, KRT, ones, pb, t0, nt, kts, OT, DN, og, Sps, PT, pi, rden, otsb, OGT4, acc):
        P, nc = self.P, self.nc
        V = nc.vector
        n = nt * 128
        c0 = t0 * 128
        nk = len(kts)

        def emitS(kt, S):
            P.op("tensor", lambda: nc.tensor.matmul(S[:, 0:n], lhsT=KNb[:, kt * 128:(kt + 1) * 128],
                                                    rhs=QNb[:, c0:c0 + n], start=True, stop=False),
                 [KNb.o, QNb.o], [S.o])
            P.op("tensor", lambda: nc.tensor.matmul(S[:, 0:n], lhsT=KRT[pb:pb + 64, kt * 128:(kt + 1) * 128],
                                                    rhs=QRc[pb:pb + 64, c0:c0 + n], start=False, stop=True),
                 [KRT.o, QRc.o], [S.o])

        def pv(ki, kt, S, pt):
            P.op("scalar", lambda: nc.scalar.activation(out=pt[:, 0:n], in_=S[:, 0:n], func=AF.Exp, scale=MLA_SCALE),
                 [S.o], [pt.o])
            P.op("tensor", lambda: nc.tensor.matmul(OT[:, 0:n], lhsT=VHb[:, kt, :], rhs=pt[:, 0:n], start=(ki == 0),
                                                    stop=(ki == nk - 1)), [VHb.o, pt.o], [OT.o])
            ac = acc[ki % 2]
            if ki < 2:
                P.op("vector", lambda: V.tensor_copy(out=ac[:, 0:n], in_=pt[:, 0:n]), [pt.o], [ac.o])
            else:
                P.op("vector", lambda: V.tensor_tensor(out=ac[:, 0:n], in0=ac[:, 0:n], in1=pt[:, 0:n], op=ALU.add),
                     [pt.o, ac.o], [ac.o])
        emitS(kts[0], Sps[0])
        if nk > 1:
            emitS(kts[1], Sps[1])
        for ki, kt in enumerate(kts):
            if ki + 2 < nk:
                emitS(kts[ki + 2], Sps[(ki + 2) % 3])
            pv(ki, kt, Sps[ki % 3], PT[pi % 5])
            pi += 1
        P.op("tensor", lambda: nc.tensor.matmul(DN[:, 0:n], lhsT=ones[:], rhs=acc[0][:, 0:n], start=True, stop=False),
             [ones.o, acc[0].o], [DN.o])
        P.op("tensor", lambda: nc.tensor.matmul(DN[:, 0:n], lhsT=ones[:], rhs=acc[1][:, 0:n], start=False, stop=True),
             [ones.o, acc[1].o], [DN.o])
        P.op("scalar", lambda: nc.scalar.activation(out=rden[:, 0:n], in_=DN[:, 0:n], func=AF.Copy), [DN.o], [rden.o])
        P.op("scalar", lambda: nc.scalar.activation(out=otsb[:, 0:n], in_=OT[:, 0:n], func=AF.Copy), [OT.o], [otsb.o])
        P.op("vector", lambda: V.reciprocal(out=rden[:, 0:n], in_=rden[:, 0:n]), [rden.o], [rden.o])
        P.op("vector", lambda: V.tensor_tensor(out=og[:, 0:nt, :].rearrange("p t j -> p (t j)"), in0=otsb[:, 0:n],
                                               in1=rden[:, 0:n], op=ALU.mult), [otsb.o, rden.o], [og.o])
        P.dma("sync", OGT4[t0:t0 + nt, :, h, :].rearrange("t p j -> p t j"), og[:, 0:nt, :], [og.o],
              self.o_OGT[t0:t0 + nt], og.o)
        return pi

    def phase_gla(self, i, j, tiles_out):
        P, nc = self.P, self.nc
        V = nc.vector
        Wb, Wo = self.wb[f"gla_in{j}"]
        w_gd = self.inp("gla_w_gate_down", [2, 2, D, 16])[j]
        w_gu = self.inp("gla_w_gate_up", [2, 2, 16, 1024])[j]
        b_g = self.inp("gla_b_gate", [2, 2, 1024])[j]
        g_head = self.inp("gla_g_head", [2, 512])
        if not hasattr(self, "QET"):
            self.QET = self.scr("QET", [2, NT, 128, 1024], BF16)
            self.KET = self.scr("KET", [2, NT, 128, 1024], BF16)
            self.KD = self.scr("KD", [2, NT, 128, 1024], BF16)
            self.Vg = self.scr("Vg", [NT, 128, D], BF16)
            self.SR = self.scr("SR", [NT, 128, D], BF16)
            self.EBL = self.scr("EBL", [NT, 2, 128, 8], F32)
            self.OB = self.scr("OB", [NT, 128, D], F32)
            mk = lambda: [Obj() for _ in range(NT)]
            self.o_g = dict(qe=[mk(), mk()], ke=[mk(), mk()], kd=[mk(), mk()], v=mk(), sr=mk(), ebl=[mk(), mk()], ob=mk())
        QET, KET, KD, Vg, SR, EBL, OB, og_ = self.QET, self.KET, self.KD, self.Vg, self.SR, self.EBL, self.OB, self.o_g
        all_tiles = list(range(NT))
        P.begin_phase()
        C = self.load_consts(["ident", "tri"])
        ident, tri = C["ident"], C["tri"]
        Wgd = P.sb("Wgd", [128, KC, 32], BF16)
        wgs = P.sb("wgs", [128, 2, KC, 16], F32)
        for d in range(2):
            P.dma("sync", wgs[:, d, :, :], w_gd[d].rearrange("(kc p) r -> p kc r", p=128), [], [wgs.o], wgs.o)
        for d in range(2):
            P.op("scalar", lambda d=d: nc.scalar.activation(out=Wgd[:, :, d * 16:d * 16 + 16], in_=wgs[:, d, :, :],
                                                            func=AF.Copy), [wgs.o], [Wgd.o])
        WguA = [P.sb(f"WguA{d}", [17, 1024], F32) for d in range(2)]
        for d in range(2):
            P.dma("sync", WguA[d][0:16, :], w_gu[d], [], [WguA[d].o], WguA[d].o)
            P.dma("sync", WguA[d][16:17, :], b_g[d:d + 1, :], [], [WguA[d].o], WguA[d].o)
        gdTs = [P.sb(f"gdT{d}", [32, 512], F32) for d in range(2)]
        for d in range(2):
            P.op("vector", lambda d=d: V.memset(gdTs[d][:], 1.0), [], [gdTs[d].o])
        q_tm = [P.sb(f"q_tm{q}", [128, 1024], BF16) for q in range(4)]
        k_tm = [P.sb(f"k_tm{q}", [128, 1024], BF16) for q in range(4)]
        vst = [P.sb(f"vst{q}", [128, D], BF16) for q in range(4)]
        srst = [P.sb(f"srst{q}", [128, D], BF16) for q in range(4)]
        spf = P.sb("spf", [128, 1024], F32)
        e1 = P.sb("e1", [128, 1024], F32)
        e2 = P.sb("e2", [128, 1024], F32)
        e3 = P.sb("e3", [128, 1024], F32)
        ebl = [P.sb(f"ebl{q}", [128, 8], F32) for q in range(2)]
        sto = [P.sb(f"sto{q}", [128, 1024], BF16) for q in range(3)]
        X = P.ps("X", [128, 1024], F32)
        Y = P.ps("Y", [128, 1024], F32)
        ptr_ref = []
        sc = [0]
        LN16 = math.log(16.0)

        def consume(st, tt, t, bi, pp):
            if bi < 2:
                P.op("scalar", lambda: nc.scalar.activation(out=q_tm[tt][:, bi * 512:(bi + 1) * 512], in_=pp[:, 0:512],
                                                            func=AF.Copy), [pp.o], [q_tm[tt].o])
            elif bi < 4:
                P.op("scalar", lambda: nc.scalar.activation(out=k_tm[tt][:, (bi - 2) * 512:(bi - 1) * 512], in_=pp[:, 0:512],
                                                            func=AF.Copy), [pp.o], [k_tm[tt].o])
            elif bi < 8:
                P.op("scalar", lambda: nc.scalar.activation(out=vst[tt][:, (bi - 4) * 512:(bi - 3) * 512], in_=pp[:, 0:512],
                                                            func=AF.Copy), [pp.o], [vst[tt].o])
                if bi == 7:
                    P.dma("scalar", Vg[t], vst[tt][:], [vst[tt].o], [og_["v"][t]], vst[tt].o)
            else:
                P.op("scalar", lambda: nc.scalar.activation(out=srst[tt][:, (bi - 8) * 512:(bi - 7) * 512], in_=pp[:, 0:512],
                                                            func=AF.Silu), [pp.o], [srst[tt].o])
                if bi == 11:
                    P.dma("scalar", SR[t], srst[tt][:], [srst[tt].o], [og_["sr"][t]], srst[tt].o)

        import os
        GDBG = int(os.environ.get("GDBG", "9"))

        def st_begin(st, aT):
            n = len(st) * 128
            if GDBG < 2:
                return
            for d in range(2):
                for kc in range(KC):
                    P.op("tensor", lambda kc=kc, d=d: nc.tensor.matmul(X[0:16, d * 512:d * 512 + n],
                                                                      lhsT=Wgd[:, kc, d * 16:(d + 1) * 16], rhs=aT[:, kc, 0:n],
                                                                      start=(kc == 0), stop=(kc == KC - 1)), [Wgd.o, aT.o], [X.o])
                P.op("scalar", lambda d=d: nc.scalar.activation(out=gdTs[d][0:16, 0:n], in_=X[0:16, d * 512:d * 512 + n],
                                                                func=AF.Copy), [X.o], [gdTs[d].o])

        def gate_dir(tt, t, d, ptr):
            for half in range(2):
                hs = slice(half * 512, (half + 1) * 512)
                P.op("tensor", lambda hs=hs: nc.tensor.matmul(X[:, hs], lhsT=gdTs[d][0:17, tt * 128:(tt + 1) * 128],
                                                              rhs=WguA[d][0:17, hs], start=True, stop=True),
                     [gdTs[d].o, WguA[d].o], [X.o])
            if GDBG < 5:
                return
            P.op("scalar", lambda: nc.scalar.activation(out=spf[:], in_=X[:], func=AF.Exp, scale=-1.0), [X.o], [spf.o])
            P.op("scalar", lambda: nc.scalar.activation(out=spf[:], in_=spf[:], func=AF.Ln, bias=1.0), [spf.o], [spf.o])
            if GDBG < 6:
                return
            for c in range(8):
                P.op("tensor", lambda c=c: nc.tensor.matmul(Y[:, c * 128:(c + 1) * 128], lhsT=spf[:, c * 128:(c + 1) * 128],
                                                            rhs=tri[:, d, :], start=True, stop=True), [spf.o, tri.o], [Y.o])
            for half in range(2):
                hs = slice(half * 512, (half + 1) * 512)
                P.op("tensor", lambda hs=hs: nc.tensor.matmul(X[:, hs], lhsT=tri[:, 2 + d, :], rhs=spf[:, hs],
                                                              start=True, stop=True), [spf.o, tri.o], [X.o])
            if GDBG < 7:
                return
            P.op("scalar", lambda: nc.scalar.activation(out=e1[:], in_=Y[:], func=AF.Exp, bias=-LN16), [Y.o], [e1.o])
            P.op("scalar", lambda: nc.scalar.activation(out=e2[:], in_=Y[:], func=AF.Exp, scale=-1.0), [Y.o], [e2.o])
            P.op("scalar", lambda: nc.scalar.activation(out=e3[:], in_=X[:], func=AF.Exp), [X.o], [e3.o])
            eb = ebl[d]
            col = 127 if d == 0 else 0
            P.op("scalar", lambda: nc.scalar.activation(out=eb[:], in_=Y[:].rearrange("p (c i) -> p c i", c=8)[:, :, col],
                                                        func=AF.Exp), [Y.o], [eb.o])
            P.dma("scalar", EBL[t, d], eb[:], [eb.o], [og_["ebl"][d][t]], eb.o)
            if GDBG < 8:
                return
            for (src, ee, dst, ob) in ((ptr[:, 0:1024], e1, QET, og_["qe"]), (ptr[:, 1024:2048], e2, KET, og_["ke"]),
                                       (k_tm[tt][:], e3, KD, og_["kd"])):
                so_ = sto[sc[0] % 3]
                sc[0] += 1
                rd = [ptr.o, ee.o] if src is not k_tm[tt] else [k_tm[tt].o, ee.o]
                P.op("vector", lambda src=src, ee=ee, so_=so_: V.tensor_tensor(out=so_[:], in0=src, in1=ee[:], op=ALU.mult),
                     [ptr.o, k_tm[tt].o, ee.o], [so_.o])
                P.dma("sync", dst[d, t], so_[:], [so_.o], [ob[d][t]], so_.o)

        def st_end(st, aT):
            ptr = ptr_ref[0]
            if GDBG < 3:
                return
            for tt, t in enumerate(st):
                for c in range(8):
                    P.op("tensor", lambda c=c, tt=tt: nc.tensor.transpose(out=ptr[:, c * 128:(c + 1) * 128],
                                                                  in_=q_tm[tt][:, c * 128:(c + 1) * 128], identity=ident[:]),
                         [q_tm[tt].o, ident.o], [ptr.o])
                    P.op("tensor", lambda c=c, tt=tt: nc.tensor.transpose(out=ptr[:, 1024 + c * 128:1024 + (c + 1) * 128],
                                                                  in_=k_tm[tt][:, c * 128:(c + 1) * 128], identity=ident[:]),
                         [k_tm[tt].o, ident.o], [ptr.o])
                for d in range(2):
                    if GDBG >= 4:
                        gate_dir(tt, t, d, ptr)

        blocks = [(c * 512, 512) for c in range(12)]
        orig_ps = P.ps

        def ps_hook(name, shape, dtype=F32):
            t_ = orig_ps(name, shape, dtype)
            if name == "ptr":
                ptr_ref.append(t_)
            return t_
        P.ps = ps_hook
        self.pa_loop(i, all_tiles, Wb, Wo, blocks, consume, per_st_begin=st_begin, per_st_end=st_end, C=C)
        P.ps = orig_ps
        P.end_phase()
        if self.stop == "pa":
            return
        P.begin_phase()
        C = self.load_consts(["ident", "masks"])
        ident, masks = C["ident"], C["masks"]
        GH = P.sb("GH", [128, 512], F32)
        P.dma("sync", GH[:], g_head[j:j + 1, :].broadcast_to([128, 512]), [], [GH.o], GH.o)
        P.op("vector", lambda: V.tensor_scalar_mul(out=GH[:], in0=GH[:], scalar1=math.sqrt(512.0)), [GH.o], [GH.o])
        S = P.sb("S", [128, 4, 2, 512], F32)
        Sbf = P.sb("Sbf", [128, 4, 2, 512], BF16, nobj=8)
        Sob = [Obj() for _ in range(8)]
        B = dict(
            qe=[P.sb(f"qe{q}", [128, 1024], BF16) for q in range(3)], ke=[P.sb(f"ke{q}", [128, 1024], BF16) for q in range(3)],
            kd=[P.sb(f"kd{q}", [128, 1024], BF16) for q in range(3)], v=[P.sb(f"v{q}", [128, D], BF16) for q in range(3)],
            ebl=[P.sb(f"eb{q}", [128, 8], F32) for q in range(3)], ob=[P.sb(f"ob{q}", [128, D], F32) for q in range(3)],
            sr=[P.sb(f"sr{q}", [128, D], BF16) for q in range(3)], og=[P.sb(f"og{q}", [128, D], BF16) for q in range(3)],
            ogT=[P.sb(f"ogT{q}", [128, KC, 128], BF16) for q in range(3)],
            am=[P.sb(f"am{q}", [128, 128], BF16) for q in range(2)], ot=[P.sb(f"ot{q}", [128, 512], F32) for q in range(2)],
            t2=[P.sb(f"t2{q}", [128, 512], F32) for q in range(2)],
            ss4=P.sb("gss4", [128, 8], F32), ssum=P.sb("gssum", [128, 1], F32), rstd=P.sb("grstd", [128, 1], F32),
            sq=P.sb("gsq", [128, 512], BF16),
            o_ps=[P.ps(f"o{q}", [128, 512], F32) for q in range(2)], at_ps=P.ps("at", [128, 512], F32),
            pk=[P.ps(f"pk{q}", [128, 512], F32) for q in range(2)], ptr=P.ps("ptr", [128, D], BF16))
        outset = set(tiles_out)
        for d, order, fwd in ((1, [1, 0] + list(range(NT - 1, NTC - 1, -1)), False), (0, list(range(NT)), True)):
            P.op("vector", lambda: V.memset(S[:], 0.0), [], [S.o])
            P.op("vector", lambda: V.memset(Sbf[:], 0.0), [], list(Sbf.o))

            def loads(k, t):
                b = k % 3
                P.dma("sync", B["qe"][b][:], QET[d, t], [og_["qe"][d][t]], [B["qe"][b].o], B["qe"][b].o)
                P.dma("sync", B["ke"][b][:], KET[d, t], [og_["ke"][d][t]], [B["ke"][b].o], B["ke"][b].o)
                P.dma("sync", B["kd"][b][:], KD[d, t], [og_["kd"][d][t]], [B["kd"][b].o], B["kd"][b].o)
                P.dma("sync", B["v"][b][:], Vg[t], [og_["v"][t]], [B["v"][b].o], B["v"][b].o)
                P.dma("sync", B["ebl"][b][:], EBL[t, d], [og_["ebl"][d][t]], [B["ebl"][b].o], B["ebl"][b].o)
                if fwd and t in outset:
                    P.dma("sync", B["ob"][b][:], OB[t], [og_["ob"][t]], [B["ob"][b].o], B["ob"][b].o)
                    P.dma("sync", B["sr"][b][:], SR[t], [og_["sr"][t]], [B["sr"][b].o], B["sr"][b].o)
            loads(0, order[0])
            loads(1, order[1])
            for k, t in enumerate(order):
                if k + 2 < len(order):
                    loads(k + 2, order[k + 2])
                need = t in outset
                for h in range(4):
                    self._gla_head(h, d, k, t, fwd, need, B, S, Sbf, masks, GH)
                b = k % 3
                if need and not fwd:
                    P.dma("scalar", OB[t], B["ob"][b][:], [B["ob"][b].o], [og_["ob"][t]], B["ob"][b].o)
                if need and fwd:
                    self._og_store(t, B["og"][b], B["ogT"][b], B["ptr"], ident)
        P.end_phase()

    def _og_store(self, t, og, oT, ptr, ident):
        P, nc = self.P, self.nc
        for kc in range(KC):
            P.op("tensor", lambda kc=kc: nc.tensor.transpose(out=ptr[:, kc * 128:(kc + 1) * 128],
                                                            in_=og[:, kc * 128:(kc + 1) * 128], identity=ident[:]),
                 [og.o, ident.o], [ptr.o])
        P.op("scalar", lambda: nc.scalar.activation(out=oT[:], in_=ptr[:].rearrange("p (k j) -> p k j", k=KC), func=AF.Copy),
             [ptr.o], [oT.o])
        P.dma("scalar", self.OGT[t], oT[:].rearrange("p k j -> p (k j)"), [oT.o], [self.o_OGT[t]], oT.o)

    def _gla_head(self, h, d, k, t, fwd, need, B, S, Sbf, masks, GH):
        P, nc = self.P, self.nc
        V = nc.vector
        b = k % 3
        qe, ke, kd, v, ebl = B["qe"][b], B["ke"][b], B["kd"][b], B["v"][b], B["ebl"][b]
        o_ps = B["o_ps"][(k * 4 + h) % 2]
        at_ps = B["at_ps"]
        am = B["am"][(k * 4 + h) % 2]
        vs = slice(h * 512, (h + 1) * 512)
        if need:
            for kc in range(2):
                c = h * 2 + kc
                P.op("tensor", lambda kc=kc, c=c: nc.tensor.matmul(o_ps[:, 0:512], lhsT=qe[:, c * 128:(c + 1) * 128],
                                                                  rhs=Sbf[:, h, kc, :], start=(kc == 0), stop=False),
                     [qe.o, Sbf.o[c]], [o_ps.o])
            for kc in range(2):
                c = h * 2 + kc
                P.op("tensor", lambda kc=kc, c=c: nc.tensor.matmul(at_ps[:, 0:128], lhsT=ke[:, c * 128:(c + 1) * 128],
                                                                  rhs=qe[:, c * 128:(c + 1) * 128], start=(kc == 0),
                                                                  stop=(kc == 1)), [ke.o, qe.o], [at_ps.o])
            P.op("vector", lambda: V.tensor_tensor(out=am[:], in0=at_ps[:, 0:128], in1=masks[:, d, :], op=ALU.mult),
                 [at_ps.o, masks.o], [am.o])
            P.op("tensor", lambda: nc.tensor.matmul(o_ps[:, 0:512], lhsT=am[:], rhs=v[:, vs], start=False, stop=True),
                 [am.o, v.o], [o_ps.o])
        for kc in range(2):
            c = h * 2 + kc
            pk = B["pk"][kc]
            P.op("tensor", lambda c=c, pk=pk: nc.tensor.matmul(pk[:, 0:512], lhsT=kd[:, c * 128:(c + 1) * 128], rhs=v[:, vs],
                                                              start=True, stop=True), [kd.o, v.o], [pk.o])
            P.op("vector", lambda kc=kc, c=c, pk=pk: V.scalar_tensor_tensor(
                out=S[:, h, kc, :], in0=S[:, h, kc, :], scalar=ebl[:, c:c + 1], in1=pk[:, 0:512], op0=ALU.mult, op1=ALU.add),
                [S.o, ebl.o, pk.o], [S.o])
            P.op("scalar", lambda kc=kc: nc.scalar.activation(out=Sbf[:, h, kc, :], in_=S[:, h, kc, :], func=AF.Copy),
                 [S.o], [Sbf.o[c]])
        if not need:
            return
        if not fwd:
            ob = B["ob"][b]
            P.op("scalar", lambda: nc.scalar.activation(out=ob[:, vs], in_=o_ps[:, 0:512], func=AF.Copy), [o_ps.o], [ob.o])
            return
        ob, sr, og = B["ob"][b], B["sr"][b], B["og"][b]
        ot = B["ot"][h % 2]
        t2 = B["t2"][h % 2]
        P.op("vector", lambda: V.tensor_tensor(out=ot[:], in0=o_ps[:, 0:512], in1=ob[:, vs], op=ALU.add), [o_ps.o, ob.o], [ot.o])
        self.norm_scale(ot[:], [ot.o], 512, 1, B["ss4"], B["ssum"], B["rstd"], B["sq"], 512 * EPS)
        P.op("vector", lambda: V.scalar_tensor_tensor(out=t2[:], in0=ot[:], scalar=B["rstd"][:, 0:1], in1=GH[:],
                                                      op0=ALU.mult, op1=ALU.mult), [ot.o, B["rstd"].o, GH.o], [t2.o])
        P.op("vector", lambda: V.tensor_tensor(out=og[:, vs], in0=t2[:], in1=sr[:, vs], op=ALU.mult), [t2.o, sr.o], [og.o])

    def build(self, skip_ffn=False, stop=None):
        P = self.P
        self.stop = stop
        def prep(i):
            kind, j = i % 3, i // 3
            if kind == 0:
                self.prep_nat(f"gla_in{j}", self.inp("gla_w_in", [2, D, 6144])[j], [D, 6144], pieces=8)
                self.prep_nat(f"gla_out{j}", self.inp("gla_w_out", [2, D, D])[j], [D, D])
            elif kind == 1:
                self.prep_mla(j)
            else:
                self.prep_nat("swa_in", self.inp("swa_w_in", [1, D, 2560])[j], [D, 2560])
                self.prep_nat("swa_out", self.inp("swa_w_out", [1, D, D])[j], [D, D])
            if not skip_ffn:
                self.prep_ffn(i)
        PREPALL = True
        for i_ in (self.layers if PREPALL else self.layers[:1]):
            prep(i_)
        self.phase_init()
        self.phase_mod(self.layers)
        lat_tiles = list(range(NTC, NT))
        all_tiles = list(range(NT))
        for li, i in enumerate(self.layers):
            if li + 1 < len(self.layers) and not PREPALL:
                self.prep_gate = [self.o_modv[i]] if os.environ.get("NOGATE") is None else []
                prep(self.layers[li + 1])
            if stop == "mod":
                break
            kind, j = i % 3, i // 3
            last = (i == self.final_layer)
            tiles_out = lat_tiles if last else all_tiles
            if kind == 0:
                self.phase_gla(i, j, tiles_out)
                wo = self.wb[f"gla_out{j}"]
            elif kind == 1:
                self.phase_mla(i, j, tiles_out)
                wo = self.wb["mla_out"]
            else:
                self.phase_swa(i, j, all_tiles)
                wo = self.wb["swa_out"]
            if stop in ("mix", "pa"):
                break
            self.phase_post(i, wo[0], wo[1], tiles_out)
            if not skip_ffn:
                self.phase_ffn(i, tiles_out, to_out=last)
        P.barrier()
        nc = P.finish()
        P.root.close()
        return nc


def _prep_inputs(inputs, b, names):
    f = np.ascontiguousarray
    m = {}
    cst = _consts()
    for k in names:
        if k == "x":
            m[k] = f(inputs["x"][b])
        elif k == "ctx":
            m[k] = f(inputs["ctx"][b])
        elif k == "cc":
            m[k] = f(np.stack([inputs["c"][b], inputs["c_ctx"]]))
        elif k in cst:
            m[k] = cst[k]
        else:
            m[k] = f(inputs[k])
    return m


def run(inputs, layers=(0, 1, 2, 3), debug_outs=(), final_layer=DEPTH - 1, cores=8, skip_ffn=False, trace=False,
        stop=None):
    bld = Builder(layers=layers, debug_outs=debug_outs, final_layer=final_layer)
    nc = bld.build(skip_ffn=skip_ffn, stop=stop)
    print("stats", bld.P.stats, flush=True)
    names = list(bld.in_decl.keys())
    in_maps = [_prep_inputs(inputs, b, names) for b in range(cores)]
    res = run_bass_kernel_spmd(nc, in_maps, core_ids=list(range(cores)), trace=trace)
    return res


def kernel(**inputs):
    inputs = {k: np.asarray(v) for k, v in inputs.items()}
    res = run(inputs)
    return np.stack([np.asarray(r["out"]) for r in res.results], axis=0).astype(np.float32)
```

```python
import math
import os
from contextlib import ExitStack

import numpy as np
import ml_dtypes
import concourse.bass as bass
import concourse.mybir as mybir
from concourse.bass_utils import run_bass_kernel_spmd

F32 = mybir.dt.float32
BF16 = mybir.dt.bfloat16
ALU = mybir.AluOpType
AF = mybir.ActivationFunctionType
AX = mybir.AxisListType

D = 2048
T_CTX = 256
T_LAT = 4096
T = T_CTX + T_LAT
NT = T // 128
NTC = T_CTX // 128
KC = D // 128
DFF = 5632
FC = DFF // 128
EPS = 1e-6
DEPTH = 4
SQD = math.sqrt(D)

SAME_ENGINE_SYNC = True
COMPUTE = ("tensor", "vector", "scalar", "gpsimd")
ENGINES = COMPUTE + ("sync",)
BARRIER_ENGINES = ("tensor", "vector", "scalar", "sync")


class SemSlot:
    __slots__ = ("sem", "cnt", "bg")

    def __init__(self, sem):
        self.sem = sem
        self.cnt = 0
        self.bg = False


class Obj:
    __slots__ = ("lw", "rd", "rdd", "slot", "name")

    def __init__(self, name=""):
        self.lw = -1
        self.rd = {}
        self.rdd = []
        self.slot = None
        self.name = name


class Tile:
    def __init__(self, t, o):
        self.t = t
        self.o = o

    def __getitem__(self, k):
        return self.t[k]


class Op:
    __slots__ = ("eng", "fn", "cdeps", "ddeps", "slot", "dcount", "needs", "ordv")


class Prog:
    def __init__(self):
        self.nc = bass.Bass("TRN2", target_bir_lowering=False)
        self.ops = []
        self.last = {}
        self.root = ExitStack()
        self.free_slots = []
        self.all_slots = []
        self.phase_objs = []
        self.phase = None
        self.n_sem = 0
        self.esems = {e: self.root.enter_context(self.nc.semaphore(f"e_{e}")) for e in ENGINES}

    def begin_phase(self):
        assert self.phase is None
        self.phase = ExitStack()
        self.phase_objs = []

    def end_phase(self):
        self.barrier()
        for o in self.phase_objs:
            if o.slot is not None:
                self.free_slots.append(o.slot)
                o.slot = None
        self.phase.close()
        self.phase = None

    def sb(self, name, shape, dtype, nobj=1):
        self.uid = getattr(self, "uid", 0) + 1
        name = f"s{self.uid}_{name}"
        t = self.phase.enter_context(self.nc.sbuf_tensor(name, list(shape), dtype))
        if nobj == 1:
            o = Obj(name)
            self.phase_objs.append(o)
            return Tile(t, o)
        objs = [Obj(f"{name}{i}") for i in range(nobj)]
        self.phase_objs.extend(objs)
        return Tile(t, objs)

    def ps(self, name, shape, dtype=F32):
        self.uid = getattr(self, "uid", 0) + 1
        name = f"p{self.uid}_{name}"
        nb = int(np.prod(shape[1:])) * (2 if dtype == BF16 else 4)
        assert nb % 2048 == 0, (name, shape)
        t = self.phase.enter_context(self.nc.psum_tensor(name, list(shape), dtype))
        o = Obj(name)
        self.phase_objs.append(o)
        return Tile(t, o)

    def _slot(self, o, fresh=False):
        if o.slot is None:
            if self.free_slots and not fresh:
                o.slot = self.free_slots.pop()
            else:
                self.n_sem += 1
                sem = self.root.enter_context(self.nc.semaphore(f"d{self.n_sem}"))
                o.slot = SemSlot(sem)
                self.all_slots.append(o.slot)
        return o.slot

    def _rec(self, eng, fn, reads, writes, slot):
        idx = len(self.ops)
        op = Op()
        op.eng = eng
        op.fn = fn
        op.slot = slot
        op.needs = False
        op.ordv = 0
        cd = {}
        dd = {}

        def dep(j):
            if j < 0:
                return
            pj = self.ops[j]
            if pj.slot is not None:
                if dd.get(pj.slot, 0) < pj.dcount:
                    dd[pj.slot] = pj.dcount
            else:
                if cd.get(pj.eng, -1) < j:
                    cd[pj.eng] = j

        for o in reads:
            dep(o.lw)
        for o in writes:
            dep(o.lw)
            for j in o.rd.values():
                dep(j)
            for j in o.rdd:
                dep(j)
        for P in list(cd.keys()):
            if P == eng and (P == "tensor" or not SAME_ENGINE_SYNC):
                del cd[P]
            else:
                self.ops[cd[P]].needs = True
        op.cdeps = cd
        op.ddeps = dd
        if slot is not None:
            slot.cnt += 16
            op.dcount = slot.cnt
        else:
            op.dcount = 0
        for o in reads:
            if slot is not None:
                o.rdd.append(idx)
            else:
                o.rd[eng] = idx
        for o in writes:
            o.lw = idx
            o.rd = {}
            o.rdd = []
        self.ops.append(op)
        self.last[eng] = idx
        return idx

    def op(self, eng, fn, reads=(), writes=()):
        return self._rec(eng, fn, reads, writes, None)

    def dma(self, eng, out, in_, reads, writes, semobj, bg=False, **kw):
        nc = self.nc
        slot = self._slot(semobj, fresh=bg)
        if bg:
            slot.bg = True
        e = getattr(nc, eng)
        return self._rec(eng, lambda: e.dma_start(out=out, in_=in_, **kw), reads, writes, slot)

    def barrier(self):
        lasts = {e: j for e, j in self.last.items() if e in BARRIER_ENGINES}
        for eng in BARRIER_ENGINES:
            op = Op()
            op.eng = eng
            op.fn = None
            op.slot = None
            op.needs = False
            op.ordv = 0
            op.dcount = 0
            cd = {}
            for P, j in lasts.items():
                pj = self.ops[j]
                if P == eng and P == "tensor":
                    continue
                if pj.slot is None and pj.fn is not None:
                    cd[P] = j
                    pj.needs = True
                else:
                    k = j
                    while k >= 0 and not (self.ops[k].eng == P and self.ops[k].slot is None
                                           and self.ops[k].fn is not None):
                        k -= 1
                    if k >= 0:
                        cd[P] = k
                        self.ops[k].needs = True
            op.cdeps = cd
            op.ddeps = {s: s.cnt for s in self.all_slots if s.cnt > 0 and not s.bg}
            self.ops.append(op)

    def finish(self):
        nc = self.nc
        cnt = {e: 0 for e in ENGINES}
        for op in self.ops:
            if op.slot is None and op.needs:
                cnt[op.eng] += 1
                op.ordv = cnt[op.eng]
        sems = self.esems
        seen = {e: {} for e in ENGINES}
        nw = 0
        for op in self.ops:
            E = getattr(nc, op.eng)
            sn = seen[op.eng]
            for P, j in op.cdeps.items():
                v = self.ops[j].ordv
                if sn.get(P, 0) < v:
                    E.wait_ge(sems[P], v)
                    sn[P] = v
                    nw += 1
            for s, c in op.ddeps.items():
                if sn.get(s, 0) < c:
                    E.wait_ge(s.sem, c)
                    sn[s] = c
                    nw += 1
            if op.fn is not None:
                ins = op.fn()
                if op.slot is not None:
                    ins.then_inc(op.slot.sem, 16)
                elif op.needs:
                    ins.then_inc(sems[op.eng], 1)
        self.stats = dict(n_ops=len(self.ops), n_waits=nw, counts=cnt, n_dma_sems=self.n_sem)
        return nc


STORE_ENG = os.environ.get("STORE_ENG", "gpsimd")
GLA_H, GLA_DK, GLA_DV = 4, 256, 512
MLA_H = 16
MLA_SCALE = (128 + 64) ** -0.5
SWA_SCALE = 64 ** -0.5


def _consts():
    c = {}
    c["ident"] = np.eye(128, dtype=np.float32).astype(ml_dtypes.bfloat16)
    c["identf"] = np.eye(128, dtype=np.float32)
    t = np.arange(T_LAT)
    row = (t // 64).astype(np.float32)
    col = (t % 64).astype(np.float32)
    inv = (10000.0 ** (-np.arange(16, dtype=np.float32) / 16)).astype(np.float32)
    ang = np.concatenate([row[:, None] * inv, col[:, None] * inv], axis=-1).astype(np.float32)
    c["cs"] = np.stack([np.cos(ang), np.sin(ang)], axis=1).astype(np.float32)
    j = np.arange(128)[:, None]
    i = np.arange(128)[None, :]
    le = (j <= i).astype(np.float32)
    ge = (j >= i).astype(np.float32)
    lt = (j < i).astype(np.float32)
    gt = (j > i).astype(np.float32)
    c["masks"] = np.stack([le, ge]).astype(ml_dtypes.bfloat16)
    c["tri"] = (np.stack([le, ge, gt, lt]) * (-1.0 / 16.0)).astype(np.float32)
    c["ones_bf"] = np.ones((128, 128), dtype=np.float32).astype(ml_dtypes.bfloat16)
    c["ones_f"] = np.ones((1, 128), dtype=np.float32)
    c["ones_f128"] = np.ones((128, 128), dtype=np.float32)
    return c


class Builder:
    def __init__(self, layers=(0, 1, 2, 3), debug_outs=(), final_layer=DEPTH - 1):
        self.P = Prog()
        self.nc = self.P.nc
        self.layers = list(layers)
        self.debug_outs = set(debug_outs)
        self.final_layer = final_layer
        self.in_decl = {}
        nc = self.nc
        self.out = nc.dram_tensor("out", [T_LAT, D], F32, kind="ExternalOutput").ap()
        self.MODV = self.scr("MODV", [DEPTH, 2, 6, D], F32)
        self.H = self.scr("H", [T, D], F32)
        self.A2T = self.scr("A2T", [NT, 128, D], BF16)
        self.OGT = self.scr("OGT", [NT, 128, D], BF16)
        self.o_modv = [Obj(f"modv{i}") for i in range(DEPTH)]
        self.o_H = [Obj(f"H{t}") for t in range(NT)]
        self.o_A2T = [Obj(f"A2T{t}") for t in range(NT)]
        self.o_OGT = [Obj(f"OGT{t}") for t in range(NT)]
        self.o_out = [Obj(f"out{t}") for t in range(NT)]
        self.wb = {}
        self.nscr = 0
        self.prep_gate = []

    def inp(self, name, shape, dt=F32):
        if name not in self.in_decl:
            self.in_decl[name] = self.nc.dram_tensor(name, list(shape), dt, kind="ExternalInput").ap()
        return self.in_decl[name]

    def scr(self, name, shape, dt):
        kind = "ExternalOutput" if name in self.debug_outs else "Internal"
        return self.nc.dram_tensor(name, list(shape), dt, kind=kind).ap()

    def prep_nat(self, key, src, shape, pieces=4):
        P = self.P
        K, N = shape
        dst = self.scr(f"wb_{key}", [K, N], BF16)
        objs = []
        step = K // pieces
        so = Obj(f"wbs_{key}")
        for q in range(pieces):
            o = Obj(f"wb_{key}{q}")
            P.dma("gpsimd", dst[q * step:(q + 1) * step, :], src[q * step:(q + 1) * step, :], list(self.prep_gate), [o], so, bg=True)
            objs.append(o)
        self.wb[key] = (dst, objs)
        return dst, objs

    def prep_ffn(self, i):
        P = self.P
        w_in = self.inp("w_ffn_in", [DEPTH, D, 2 * DFF])[i]
        w_out = self.inp("w_ffn_out", [DEPTH, DFF, D])[i]
        wbin = self.scr(f"wbin{i}", [FC, 128, KC, 2, 128], BF16)
        wbout = self.scr(f"wbout{i}", [KC, 128, FC, 128], BF16)
        oin, oout = [], []
        so_in, so_out = Obj(f"wbins{i}"), Obj(f"wbouts{i}")
        for kc in range(KC):
            for g in range(2):
                o = Obj(f"wbin{i}_{kc}_{g}")
                P.dma("gpsimd", wbin[:, :, kc, g, :].rearrange("j p c -> p j c"),
                      w_in[kc * 128:(kc + 1) * 128, g * DFF:(g + 1) * DFF].rearrange("p (j c) -> p j c", c=128),
                      list(self.prep_gate), [o], so_in, bg=True)
                oin.append(o)
        for kc in range(FC):
            o = Obj(f"wbout{i}_{kc}")
            P.dma("gpsimd", wbout[:, :, kc, :].rearrange("n p c -> p n c"),
                  w_out[kc * 128:(kc + 1) * 128, :].rearrange("p (n c) -> p n c", c=128),
                  list(self.prep_gate), [o], so_out, bg=True)
            oout.append(o)
        self.wb[f"ffn{i}"] = (wbin, oin, wbout, oout)

    def load_consts(self, want):
        P = self.P
        c = {}
        if "ident" in want:
            c["ident"] = P.sb("ident", [128, 128], BF16)
            P.dma("sync", c["ident"][:], self.inp("ident", [128, 128], BF16), [], [c["ident"].o], c["ident"].o)
        if "identf" in want:
            c["identf"] = P.sb("identf", [128, 128], F32)
            P.dma("sync", c["identf"][:], self.inp("identf", [128, 128], F32), [], [c["identf"].o], c["identf"].o)
        if "cs" in want:
            c["cs"] = P.sb("cs", [128, T_LAT // 128, 2, 32], F32)
            P.dma("sync", c["cs"][:], self.inp("cs", [T_LAT, 2, 32], F32).rearrange("(n p) a f -> p n a f", p=128),
                  [], [c["cs"].o], c["cs"].o)
        if "masks" in want:
            c["masks"] = P.sb("masks", [128, 2, 128], BF16)
            P.dma("sync", c["masks"][:], self.inp("masks", [2, 128, 128], BF16).rearrange("m p i -> p m i"),
                  [], [c["masks"].o], c["masks"].o)
        if "tri" in want:
            c["tri"] = P.sb("tri", [128, 4, 128], F32)
            P.dma("sync", c["tri"][:], self.inp("tri", [4, 128, 128], F32).rearrange("m p i -> p m i"),
                  [], [c["tri"].o], c["tri"].o)
        if "ones_bf" in want:
            c["ones_bf"] = P.sb("ones_bf", [128, 128], BF16)
            P.dma("sync", c["ones_bf"][:], self.inp("ones_bf", [128, 128], BF16), [], [c["ones_bf"].o], c["ones_bf"].o)
        if "ones_f128" in want:
            c["ones_f128"] = P.sb("ones_f128", [128, 128], F32)
            P.dma("sync", c["ones_f128"][:], self.inp("ones_f128", [128, 128], F32), [], [c["ones_f128"].o], c["ones_f128"].o)
        if "ones_f" in want:
            c["ones_f"] = P.sb("ones_f", [1, 128], F32)
            P.dma("sync", c["ones_f"][:], self.inp("ones_f", [1, 128], F32), [], [c["ones_f"].o], c["ones_f"].o)
        return c

    def bc_load(self, tile, layer, r, v):
        self.P.dma("sync", tile[:], self.MODV[layer, r, v:v + 1, :].broadcast_to([128, D]),
                   [self.o_modv[layer]], [tile.o], tile.o)

    def norm_scale(self, src_ap, src_objs, width, nseg, ss4, ssum, rstd, sq, eps_scaled):
        P, nc = self.P, self.nc
        seg = width // nseg
        P.op("vector", lambda: nc.vector.memset(ss4[:, 0:nseg], 0.0), [], [ss4.o])
        for s in range(nseg):
            P.op("scalar", lambda s=s: nc.scalar.activation(
                out=sq[:, s * seg:(s + 1) * seg], in_=src_ap[:, s * seg:(s + 1) * seg], func=AF.Square,
                accum_out=ss4[:, s:s + 1]), list(src_objs) + [ss4.o], [sq.o, ss4.o])
        if nseg > 1:
            P.op("vector", lambda: nc.vector.reduce_sum(out=ssum[:, 0:1], in_=ss4[:, 0:nseg], axis=AX.X),
                 [ss4.o], [ssum.o])
            s_in = ssum
        else:
            s_in = ss4
        P.op("vector", lambda: nc.vector.tensor_scalar_add(out=rstd[:, 0:1], in0=s_in[:, 0:1], scalar1=eps_scaled),
             [s_in.o], [rstd.o])
        P.op("scalar", lambda: nc.scalar.activation(out=rstd[:, 0:1], in_=rstd[:, 0:1], func=AF.Sqrt),
             [rstd.o], [rstd.o])
        P.op("vector", lambda: nc.vector.reciprocal(out=rstd[:, 0:1], in_=rstd[:, 0:1]), [rstd.o], [rstd.o])

    def adaln(self, h, G, SH, B, aT_view, aT_objs, ident):
        P, nc = self.P, self.nc
        self.norm_scale(h[:], [h.o], D, 1, B["ss4"], B["ssum"], B["rstd"], B["sq"], D * EPS)
        tmp, abf, ptr = B["tmp"], B["abf"], B["ptr"]
        P.op("vector", lambda: nc.vector.scalar_tensor_tensor(out=tmp[:], in0=h[:], scalar=B["rstd"][:, 0:1], in1=G[:],
                                                              op0=ALU.mult, op1=ALU.mult),
             [h.o, B["rstd"].o, G.o], [tmp.o])
        P.op("vector", lambda: nc.vector.tensor_tensor(out=abf[:], in0=tmp[:], in1=SH[:], op=ALU.add),
             [tmp.o, SH.o], [abf.o])
        for kc in range(KC):
            P.op("tensor", lambda kc=kc: nc.tensor.transpose(out=ptr[:, kc * 128:(kc + 1) * 128],
                                                            in_=abf[:, kc * 128:(kc + 1) * 128], identity=ident[:]),
                 [abf.o, ident.o], [ptr.o])
        P.op("scalar", lambda: nc.scalar.activation(out=aT_view, in_=ptr[:].rearrange("p (k j) -> p k j", k=KC),
                                                    func=AF.Copy), [ptr.o], list(aT_objs))

    def norm_bufs(self):
        P = self.P
        return dict(ss4=P.sb("ss4", [128, 8], F32), ssum=P.sb("ssum", [128, 1], F32), rstd=P.sb("rstd", [128, 1], F32),
                    sq=P.sb("sq", [128, D], BF16), tmp=P.sb("tmp", [128, D], F32), abf=P.sb("abf", [128, D], BF16),
                    ptr=P.ps("ptr", [128, D], BF16))

    def rope(self, X, x_objs, cs, n, O, o_obj, H, tmps):
        P, nc = self.P, self.nc
        cos = cs[:, n, 0, :].unsqueeze(1).broadcast_to([128, H, 32])
        sin = cs[:, n, 1, :].unsqueeze(1).broadcast_to([128, H, 32])
        t1, t2 = tmps
        a = t1[:, 0:H * 32].rearrange("p (h f) -> p h f", h=H)
        b = t2[:, 0:H * 32].rearrange("p (h f) -> p h f", h=H)
        V = nc.vector
        rd = list(x_objs) + [cs.o]
        P.op("vector", lambda: V.tensor_tensor(out=a, in0=X[:, :, 0, :], in1=cos, op=ALU.mult), rd, [t1.o])
        P.op("vector", lambda: V.tensor_tensor(out=b, in0=X[:, :, 1, :], in1=sin, op=ALU.mult), rd, [t2.o])
        P.op("vector", lambda: V.tensor_tensor(out=O[:, :, 0, :], in0=a, in1=b, op=ALU.subtract), [t1.o, t2.o], [o_obj])
        P.op("vector", lambda: V.tensor_tensor(out=a, in0=X[:, :, 1, :], in1=cos, op=ALU.mult), rd + [o_obj], [t1.o])
        P.op("vector", lambda: V.tensor_tensor(out=b, in0=X[:, :, 0, :], in1=sin, op=ALU.mult), rd + [o_obj], [t2.o])
        P.op("vector", lambda: V.tensor_tensor(out=O[:, :, 1, :], in0=a, in1=b, op=ALU.add), [t1.o, t2.o], [o_obj])

    @staticmethod
    def supertiles(tiles, n=4):
        out = []
        cur = []
        for t in tiles:
            if cur and (len(cur) == n or (cur[-1] < NTC) != (t < NTC) or t != cur[-1] + 1):
                out.append(cur)
                cur = []
            cur.append(t)
        if cur:
            out.append(cur)
        return out

    def phase_init(self):
        P = self.P
        x = self.inp("x", [T_LAT, D])
        ctx = self.inp("ctx", [T_CTX, D])
        o = Obj("init")
        P.dma("sync", self.H[0:T_CTX, :], ctx, [], self.o_H[0:NTC], o)
        for q in range(4):
            a, b = q * 1024, (q + 1) * 1024
            P.dma("sync", self.H[T_CTX + a:T_CTX + b, :], x[a:b, :], [], self.o_H[NTC + q * 8:NTC + (q + 1) * 8], o)

    def phase_mod(self, mod_layers):
        P, nc = self.P, self.nc
        w_ada = self.inp("w_ada", [DEPTH, D, 6 * D])
        b_ada = self.inp("b_ada", [DEPTH, 6 * D])
        g_norm = self.inp("g_norm", [DEPTH, 4, D])
        ccd = self.inp("cc", [2, D])
        P.begin_phase()
        cc = P.sb("cc", [128, 2, KC], F32)
        sc = P.sb("sc", [128, 2, KC], F32)
        wb = [P.sb(f"wada{i}", [128, KC, 512], F32) for i in range(2)]
        bb = [P.sb(f"ba{i}", [2, 512], F32) for i in range(2)]
        gn = P.sb("gn", [2, 4, D], F32)
        msb = P.sb("msb", [2, 6, D], F32)
        mps = [P.ps(f"mps{i}", [128, 512], F32) for i in range(2)]
        V = nc.vector
        P.dma("sync", cc[:], ccd.rearrange("r (kc p) -> p r kc", p=128), [], [cc.o], cc.o,
              allow_slow_non_contiguous=True)
        P.op("scalar", lambda: nc.scalar.activation(out=sc[:], in_=cc[:], func=AF.Silu), [cc.o], [sc.o])
        k = 0
        for i in mod_layers:
            P.dma("sync", gn[:].rearrange("p a d -> p (a d)"),
                  g_norm[i:i + 1].rearrange("o a d -> o (a d)").broadcast_to([2, 4 * D]), [], [gn.o], gn.o)
            for n in range(24):
                w = wb[k % 2]
                mp = mps[k % 2]
                ba = bb[k % 2]
                k += 1
                P.dma("sync", w[:], w_ada[i][:, n * 512:(n + 1) * 512].rearrange("(kc p) n -> p kc n", p=128),
                      [], [w.o], w.o)
                P.dma("sync", ba[:], b_ada[i:i + 1, n * 512:(n + 1) * 512].broadcast_to([2, 512]), [], [ba.o], ba.o)
                for kc in range(KC):
                    P.op("tensor", lambda mp=mp, w=w, kc=kc: nc.tensor.matmul(
                        mp[0:2, :], lhsT=sc[:, :, kc], rhs=w[:, kc, :], start=(kc == 0), stop=(kc == KC - 1)),
                        [sc.o, w.o], [mp.o])
                j, off = divmod(n * 512, D)
                P.op("vector", lambda mp=mp, j=j, off=off, ba=ba: V.tensor_tensor(
                    out=msb[:, j, off:off + 512], in0=mp[0:2, :], in1=ba[:], op=ALU.add), [mp.o, ba.o], [msb.o])
            for (v, gi, addone) in ((1, 0, True), (2, 1, False), (4, 2, True), (5, 3, False)):
                if addone:
                    P.op("vector", lambda v=v, gi=gi: V.scalar_tensor_tensor(
                        out=msb[:, v, :], in0=msb[:, v, :], scalar=1.0, in1=gn[:, gi, :], op0=ALU.add, op1=ALU.mult),
                        [msb.o, gn.o], [msb.o])
                else:
                    P.op("vector", lambda v=v, gi=gi: V.tensor_tensor(
                        out=msb[:, v, :], in0=msb[:, v, :], in1=gn[:, gi, :], op=ALU.mult), [msb.o, gn.o], [msb.o])
                P.op("vector", lambda v=v: V.tensor_scalar_mul(out=msb[:, v, :], in0=msb[:, v, :], scalar1=SQD),
                     [msb.o], [msb.o])
            P.dma("sync", self.MODV[i].rearrange("r v d -> r (v d)"), msb[:].rearrange("p v d -> p (v d)"),
                  [msb.o], [self.o_modv[i]], msb.o)
        P.end_phase()

    def pa_loop(self, i, tiles, Wb, Wobjs, blocks, consume, per_st_begin=None, per_st_end=None, extra_ps=2, C=None):
        P, nc = self.P, self.nc
        B = self.norm_bufs()
        ident = C["ident"]
        G1 = P.sb("G1", [128, D], F32)
        SH1 = P.sb("SH1", [128, D], F32)
        hb = [P.sb(f"h{q}", [128, D], F32) for q in range(2)]
        aTs = [P.sb(f"aT{q}", [128, KC, 512], BF16) for q in range(2)]
        wblk = [P.sb(f"wblk{q}", [128, KC, 512], BF16) for q in range(2)]
        pps = [P.ps(f"pp{q}", [128, 512], F32) for q in range(extra_ps)]
        Wv = Wb.rearrange("(kc p) n -> p kc n", p=128)
        cur_r = None
        kk = 0
        pk = 0
        for sti_, st in enumerate(self.supertiles(tiles)):
            aT = aTs[sti_ % 2]
            r = 1 if st[0] < NTC else 0
            if r != cur_r:
                self.bc_load(G1, i, r, 1)
                self.bc_load(SH1, i, r, 0)
                cur_r = r
            for tt, t in enumerate(st):
                h = hb[kk % 2]
                kk += 1
                P.dma("sync", h[:], self.H[t * 128:(t + 1) * 128, :], [self.o_H[t]], [h.o], h.o)
                self.adaln(h, G1, SH1, B, aT[:, :, tt * 128:(tt + 1) * 128], [aT.o], ident)
            if per_st_begin:
                per_st_begin(st, aT)
            import os
            DBG = int(os.environ.get("KDBG", "9"))
            if DBG < 2:
                continue
            for bi, (c0, wd) in enumerate(blocks):
                w = wblk[bi % 2]
                P.dma("sync", w[:, :, 0:wd], Wv[:, :, c0:c0 + wd], list(Wobjs), [w.o], w.o)
                for tt, t in enumerate(st):
                    pp = pps[pk % extra_ps]
                    pk += 1
                    for kc in range(KC):
                        P.op("tensor", lambda pp=pp, w=w, kc=kc, tt=tt, wd=wd, aT=aT: nc.tensor.matmul(
                            pp[:, 0:wd], lhsT=aT[:, kc, tt * 128:(tt + 1) * 128], rhs=w[:, kc, 0:wd],
                            start=(kc == 0), stop=(kc == KC - 1)), [aT.o, w.o], [pp.o])
                    if DBG >= 3:
                        consume(st, tt, t, bi, pp)
            if per_st_end:
                per_st_end(st, aT)
        return B, pps

    def phase_post(self, i, Wb, Wobjs, tiles):
        P, nc = self.P, self.nc
        P.begin_phase()
        C = self.load_consts(["ident"])
        ident = C["ident"]
        Bs = [self.norm_bufs() for _ in range(2)]
        B1s = [dict(ss4=P.sb("ss4b", [128, 8], F32), ssum=P.sb("ssumb", [128, 1], F32), rstd=P.sb("rstdb", [128, 1], F32),
                    sq=Bs[q]["sq"]) for q in range(2)]
        wout = P.sb("wout", [128, KC, D], BF16)
        Wv = Wb.rearrange("(kc p) n -> p kc n", p=128)
        for q in range(4):
            P.dma("sync", wout[:, q * 4:(q + 1) * 4, :], Wv[:, q * 4:(q + 1) * 4, :], list(Wobjs), [wout.o], wout.o)
        GG1 = P.sb("GG1", [128, D], F32)
        G2 = P.sb("G2", [128, D], F32)
        SH2 = P.sb("SH2", [128, D], F32)
        ogb = [P.sb(f"og{q}", [128, KC, 128], BF16) for q in range(2)]
        hb = [P.sb(f"h{q}", [128, D], F32) for q in range(2)]
        t5 = [P.sb(f"t5{q}", [128, 512], F32) for q in range(2)]
        a2b = [P.sb(f"a2T{q}", [128, KC, 128], BF16) for q in range(2)]
        y = P.ps("y", [128, D], F32)
        V = nc.vector
        cur_r = None
        for k, t in enumerate(tiles):
            r = 1 if t < NTC else 0
            if r != cur_r:
                self.bc_load(GG1, i, r, 2)
                self.bc_load(G2, i, r, 4)
                self.bc_load(SH2, i, r, 3)
                cur_r = r
            og, h, a2 = ogb[k % 2], hb[k % 2], a2b[k % 2]
            B = Bs[k % 2]
            B1 = B1s[k % 2]
            P.dma("sync", og[:].rearrange("p k j -> p (k j)"), self.OGT[t], [self.o_OGT[t]], [og.o], og.o)
            P.dma("sync", h[:], self.H[t * 128:(t + 1) * 128, :], [self.o_H[t]], [h.o], h.o)
            for nb in range(4):
                for kc in range(KC):
                    P.op("tensor", lambda nb=nb, kc=kc, og=og: nc.tensor.matmul(
                        y[:, nb * 512:(nb + 1) * 512], lhsT=og[:, kc, :], rhs=wout[:, kc, nb * 512:(nb + 1) * 512],
                        start=(kc == 0), stop=(kc == KC - 1)), [og.o, wout.o], [y.o])
            self.norm_scale(y[:], [y.o], D, 4, B1["ss4"], B1["ssum"], B1["rstd"], B1["sq"], D * EPS)
            for nb in range(4):
                tq = t5[nb % 2]
                sl = slice(nb * 512, (nb + 1) * 512)
                P.op("vector", lambda tq=tq, sl=sl, B1=B1: V.scalar_tensor_tensor(
                    out=tq[:], in0=y[:, sl], scalar=B1["rstd"][:, 0:1], in1=GG1[:, sl], op0=ALU.mult, op1=ALU.mult),
                    [y.o, B1["rstd"].o, GG1.o], [tq.o])
                P.op("vector", lambda tq=tq, sl=sl, h=h: V.tensor_tensor(out=h[:, sl], in0=h[:, sl], in1=tq[:], op=ALU.add),
                     [tq.o, h.o], [h.o])
            P.dma("sync", self.H[t * 128:(t + 1) * 128, :], h[:], [h.o], [self.o_H[t]], h.o)
            self.adaln(h, G2, SH2, B, a2[:], [a2.o], ident)
            P.dma("scalar", self.A2T[t], a2[:].rearrange("p k j -> p (k j)"), [a2.o], [self.o_A2T[t]], a2.o)
        P.end_phase()

    def phase_ffn(self, i, tiles, to_out):
        P, nc = self.P, self.nc
        wbin, oin, wbout, oout = self.wb[f"ffn{i}"]
        P.begin_phase()
        C = self.load_consts(["identf"])
        identf = C["identf"]
        a2 = P.sb("a2st", [128, KC, 512], BF16)
        hid = P.sb("hid", [128, FC, 512], BF16, nobj=FC)
        wi = [P.sb(f"wi{q}", [128, KC, 2, 128], BF16) for q in range(5)]
        wo = [P.sb(f"wo{q}", [128, FC, 128], BF16) for q in range(3)]
        yT = P.sb("yT", [128, KC, 512], F32, nobj=KC)
        sg = [P.sb(f"sg{q}", [128, 512], F32) for q in range(2)]
        hb = [P.sb(f"h{q}", [128, D], F32) for q in range(2)]
        t5 = [P.sb(f"t5{q}", [128, 512], F32) for q in range(2)]
        GG2 = P.sb("GG2", [128, D], F32)
        ss4 = P.sb("ss4", [128, 8], F32)
        ssum = P.sb("ssum", [128, 1], F32)
        rstd = P.sb("rstd", [128, 1], F32)
        sq = P.sb("sq", [128, D], BF16)
        pb = [P.ps(f"pb{q}", [128, 512], F32) for q in range(4)]
        ytm = P.ps("ytm", [128, D], F32)
        V = nc.vector
        cur_r = None
        kk = 0
        for st in self.supertiles(tiles):
            n = len(st) * 128
            r = 1 if st[0] < NTC else 0
            if r != cur_r:
                self.bc_load(GG2, i, r, 5)
                cur_r = r
            for tt, t in enumerate(st):
                P.dma("sync", a2[:, :, tt * 128:(tt + 1) * 128], self.A2T[t].rearrange("p (k j) -> p k j", k=KC),
                      [self.o_A2T[t]], [a2.o], a2.o)
            for j in range(FC):
                w = wi[j % 5]
                P.dma("sync", w[:].rearrange("p k g c -> p (k g c)"), wbin[j].rearrange("p k g c -> p (k g c)"),
                      list(oin), [w.o], w.o)
                gp, up = pb[(j % 2) * 2], pb[(j % 2) * 2 + 1]
                for g_, pp in ((0, gp), (1, up)):
                    for kc in range(KC):
                        P.op("tensor", lambda g_=g_, pp=pp, kc=kc, w=w, n=n: nc.tensor.matmul(
                            pp[:, 0:n], lhsT=w[:, kc, g_, :], rhs=a2[:, kc, 0:n], start=(kc == 0), stop=(kc == KC - 1)),
                            [w.o, a2.o], [pp.o])
                s_ = sg[j % 2]
                P.op("scalar", lambda s_=s_, gp=gp, n=n: nc.scalar.activation(out=s_[:, 0:n], in_=gp[:, 0:n], func=AF.Silu),
                     [gp.o], [s_.o])
                P.op("vector", lambda s_=s_, up=up, j=j, n=n: V.tensor_tensor(out=hid[:, j, 0:n], in0=s_[:, 0:n],
                                                                            in1=up[:, 0:n], op=ALU.mult),
                     [s_.o, up.o], [hid.o[j]])
            for c in range(KC):
                w = wo[c % 3]
                P.dma("sync", w[:].rearrange("p k c -> p (k c)"), wbout[c].rearrange("p k c -> p (k c)"),
                      list(oout), [w.o], w.o)
                pp = pb[c % 4]
                for kc in range(FC):
                    P.op("tensor", lambda pp=pp, w=w, kc=kc, n=n: nc.tensor.matmul(
                        pp[:, 0:n], lhsT=w[:, kc, :], rhs=hid[:, kc, 0:n], start=(kc == 0), stop=(kc == FC - 1)),
                        [w.o, hid.o[kc]], [pp.o])
                P.op("scalar", lambda pp=pp, c=c, n=n: nc.scalar.activation(out=yT[:, c, 0:n], in_=pp[:, 0:n], func=AF.Copy),
                     [pp.o], [yT.o[c]])
            for tt, t in enumerate(st):
                h = hb[kk % 2]
                kk += 1
                P.dma("sync", h[:], self.H[t * 128:(t + 1) * 128, :], [self.o_H[t]], [h.o], h.o)
                for c in range(KC):
                    P.op("tensor", lambda c=c, tt=tt: nc.tensor.transpose(
                        out=ytm[:, c * 128:(c + 1) * 128], in_=yT[:, c, tt * 128:(tt + 1) * 128], identity=identf[:]),
                        [yT.o[c], identf.o], [ytm.o])
                Bn = dict(ss4=ss4, ssum=ssum, rstd=rstd, sq=sq)
                self.norm_scale(ytm[:], [ytm.o], D, 4, ss4, ssum, rstd, sq, D * EPS)
                for nb in range(4):
                    tq = t5[nb % 2]
                    sl = slice(nb * 512, (nb + 1) * 512)
                    P.op("vector", lambda tq=tq, sl=sl: V.scalar_tensor_tensor(
                        out=tq[:], in0=ytm[:, sl], scalar=rstd[:, 0:1], in1=GG2[:, sl], op0=ALU.mult, op1=ALU.mult),
                        [ytm.o, rstd.o, GG2.o], [tq.o])
                    P.op("vector", lambda tq=tq, sl=sl, h=h: V.tensor_tensor(out=h[:, sl], in0=h[:, sl], in1=tq[:], op=ALU.add),
                         [tq.o, h.o], [h.o])
                if to_out and t >= NTC:
                    n_ = t - NTC
                    P.dma("sync", self.out[n_ * 128:(n_ + 1) * 128, :], h[:], [h.o], [self.o_out[t]], h.o)
                else:
                    P.dma("sync", self.H[t * 128:(t + 1) * 128, :], h[:], [h.o], [self.o_H[t]], h.o)
        P.end_phase()

    def phase_swa(self, i, j, tiles):
        P, nc = self.P, self.nc
        sinks = self.inp("swa_sinks", [1, 32])
        Wb, Wo = self.wb["swa_in"]
        if not hasattr(self, "QT"):
            self.QT = self.scr("QT", [4, NT, 64, 1024], BF16)
            self.o_QT = [Obj(f"QT{t}") for t in range(NT)]
        QT, o_QT = self.QT, self.o_QT
        P.begin_phase()
        C = self.load_consts(["ident", "cs", "masks"])
        ident, cs, masks = C["ident"], C["cs"], C["masks"]
        V = nc.vector
        KT = P.sb("KT", [64, 4, T], BF16)
        VA = P.sb("VA", [128, NT, 4, 72], BF16)
        P.op("vector", lambda: V.memset(VA[:], 1.0), [], [VA.o])
        es = P.sb("es", [128, 32], F32)
        P.dma("sync", es[:], sinks[0:1, :].broadcast_to([128, 32]), [], [es.o], es.o)
        P.op("scalar", lambda: nc.scalar.activation(out=es[:], in_=es[:], func=AF.Exp), [es.o], [es.o])
        qr = [P.sb(f"qr{q}", [128, 512], BF16) for q in range(2)]
        kr = P.sb("kr", [128, 256], BF16)
        tmps = (P.sb("rt1", [128, 256], F32), P.sb("rt2", [128, 256], F32))
        ptq = P.ps("ptq", [64, 1024], BF16)
        ptk = P.ps("ptk", [64, 1024], BF16)
        qTs = [P.sb(f"qTs{q}", [64, 1024], BF16) for q in range(2)]
        cnt = [0]

        import os
        DBG = int(os.environ.get("KDBG", "9"))

        def consume(st, tt, t, bi, pp):
            lat = t >= NTC
            n = t - NTC
            if DBG == 3 and bi >= 4:
                return
            if DBG in (4, 6, 7, 8) and bi < 4:
                return
            if DBG == 5:
                lat = False
            if bi < 4:
                q_ = qr[cnt[0] % 2]
                qs = qTs[cnt[0] % 2]
                cnt[0] += 1
                if lat:
                    X = pp[:, 0:512].rearrange("p (h a f) -> p h a f", h=8, a=2)
                    O = q_[:, 0:512].rearrange("p (h a f) -> p h a f", h=8, a=2)
                    self.rope(X, [pp.o], cs, n, O, q_.o, 8, tmps)
                else:
                    P.op("scalar", lambda: nc.scalar.activation(out=q_[:], in_=pp[:, 0:512], func=AF.Copy), [pp.o], [q_.o])
                for hh in range(8):
                    P.op("tensor", lambda hh=hh: nc.tensor.transpose(out=ptq[:, hh * 128:(hh + 1) * 128],
                                                                    in_=q_[:, hh * 64:(hh + 1) * 64], identity=ident[:]),
                         [q_.o, ident.o], [ptq.o])
                P.op("scalar", lambda: nc.scalar.activation(out=qs[:], in_=ptq[:], func=AF.Copy), [ptq.o], [qs.o])
                P.dma("scalar", QT[bi, t], qs[:], [qs.o], [o_QT[t]], qs.o)
            else:
                if lat:
                    X = pp[:, 0:256].rearrange("p (h a f) -> p h a f", h=4, a=2)
                    O = kr[:, 0:256].rearrange("p (h a f) -> p h a f", h=4, a=2)
                    self.rope(X, [pp.o], cs, n, O, kr.o, 4, tmps)
                else:
                    P.op("scalar", lambda: nc.scalar.activation(out=kr[:], in_=pp[:, 0:256], func=AF.Copy), [pp.o], [kr.o])
                if DBG == 6:
                    return
                for g in range(4):
                    P.op("tensor", lambda g=g: nc.tensor.transpose(out=ptk[:, g * 128:(g + 1) * 128],
                                                                  in_=kr[:, g * 64:(g + 1) * 64], identity=ident[:]),
                         [kr.o, ident.o], [ptk.o])
                if DBG == 7:
                    return
                P.op("scalar", lambda: nc.scalar.activation(out=KT[:, :, t * 128:(t + 1) * 128],
                                                            in_=ptk[:, 0:512].rearrange("p (g j) -> p g j", g=4), func=AF.Copy),
                     [ptk.o], [KT.o])
                if DBG == 8:
                    return
                P.op("scalar", lambda: nc.scalar.activation(out=VA[:, t, :, 0:64],
                                                            in_=pp[:, 256:512].rearrange("p (g d) -> p g d", g=4),
                                                            func=AF.Copy), [pp.o], [VA.o])

        blocks = [(0, 512), (512, 512), (1024, 512), (1536, 512), (2048, 512)]
        B, pps = self.pa_loop(i, tiles, Wb, Wo, blocks, consume, C=C)
        if self.stop == "pa":
            P.end_phase()
            return
        PT = [P.sb(f"PT{q}", [128, 5, 1024], BF16) for q in range(1)]
        ogt = [P.sb(f"ogt{q}", [128, D], BF16) for q in range(1)]
        ogT = [P.sb(f"ogT{q}", [128, KC, 128], BF16) for q in range(1)]
        den = P.sb("den", [128, 8], F32)
        rden = P.sb("rden", [128, 8], F32)
        ops_ = [P.ps(f"ops{q}", [128, 4, 128], F32) for q in range(2)]
        ptr = B["ptr"]
        gk = 0
        for k, t in enumerate(tiles):
            lat = t >= NTC
            n = t - NTC
            kts = [(0, None), (1, None)]
            if lat:
                if n > 0:
                    kts.append((t - 1, 1))
                kts.append((t, None))
                if n < T_LAT // 128 - 1:
                    kts.append((t + 1, 0))
            og = ogt[0]
            for g in range(4):
                qs = qTs[gk % 2]
                pt = PT[0]
                gk += 1
                P.dma("sync", qs[:], QT[g, t], [o_QT[t]], [qs.o], qs.o)
                for ki, (kt, m) in enumerate(kts):
                    for half in range(2):
                        pp = pps[half]
                        P.op("tensor", lambda pp=pp, kt=kt, half=half, g=g, qs=qs: nc.tensor.matmul(
                            pp[:, 0:512], lhsT=KT[:, g, kt * 128:(kt + 1) * 128], rhs=qs[:, half * 512:(half + 1) * 512],
                            start=True, stop=True), [KT.o, qs.o], [pp.o])
                        P.op("scalar", lambda pp=pp, pt=pt, ki=ki, half=half: nc.scalar.activation(
                            out=pt[:, ki, half * 512:(half + 1) * 512], in_=pp[:, 0:512], func=AF.Exp, scale=SWA_SCALE),
                            [pp.o], [pt.o])
                    if m is not None:
                        P.op("vector", lambda pt=pt, ki=ki, m=m: V.tensor_tensor(
                            out=pt[:, ki, :].rearrange("p (h q) -> p h q", h=8),
                            in0=pt[:, ki, :].rearrange("p (h q) -> p h q", h=8),
                            in1=masks[:, m, :].unsqueeze(1).broadcast_to([128, 8, 128]), op=ALU.mult),
                            [pt.o, masks.o], [pt.o])
                for hh in range(8):
                    ob = ops_[hh // 4]
                    for ki, (kt, m) in enumerate(kts):
                        P.op("tensor", lambda ob=ob, hh=hh, ki=ki, kt=kt, pt=pt, g=g, nk=len(kts): nc.tensor.matmul(
                            ob[:, hh % 4, 0:65], lhsT=pt[:, ki, hh * 128:(hh + 1) * 128], rhs=VA[:, kt, g, 0:65],
                            start=(ki == 0), stop=(ki == nk - 1)), [pt.o, VA.o], [ob.o])
                for half in range(2):
                    ob = ops_[half]
                    P.op("vector", lambda ob=ob, half=half, g=g: V.tensor_tensor(
                        out=den[:, half * 4:(half + 1) * 4], in0=ob[:, :, 64],
                        in1=es[:, g * 8 + half * 4:g * 8 + half * 4 + 4], op=ALU.add), [ob.o, es.o], [den.o])
                P.op("vector", lambda: V.reciprocal(out=rden[:], in_=den[:]), [den.o], [rden.o])
                for half in range(2):
                    ob = ops_[half]
                    c0 = g * 512 + half * 256
                    P.op("vector", lambda ob=ob, half=half, c0=c0, og=og: V.tensor_tensor(
                        out=og[:, c0:c0 + 256].rearrange("p (h d) -> p h d", h=4), in0=ob[:, :, 0:64],
                        in1=rden[:, half * 4:(half + 1) * 4].unsqueeze(2).broadcast_to([128, 4, 64]), op=ALU.mult),
                        [ob.o, rden.o], [og.o])
            oT = ogT[0]
            for kc in range(KC):
                P.op("tensor", lambda kc=kc, og=og: nc.tensor.transpose(out=ptr[:, kc * 128:(kc + 1) * 128],
                                                                       in_=og[:, kc * 128:(kc + 1) * 128], identity=ident[:]),
                     [og.o, ident.o], [ptr.o])
            P.op("scalar", lambda oT=oT: nc.scalar.activation(out=oT[:], in_=ptr[:].rearrange("p (k j) -> p k j", k=KC),
                                                              func=AF.Copy), [ptr.o], [oT.o])
            P.dma("scalar", self.OGT[t], oT[:].rearrange("p k j -> p (k j)"), [oT.o], [self.o_OGT[t]], oT.o)
        P.end_phase()

    def prep_mla(self, j):
        P = self.P
        self.prep_nat("mla_in", self.inp("mla_w_in", [1, D, 1088])[j], [D, 1088])
        self.prep_nat("mla_out", self.inp("mla_w_out", [1, D, D])[j], [D, D])
        wuq = self.inp("mla_w_uq", [1, 512, 3072])[j]
        wukv = self.inp("mla_w_ukv", [1, 512, 4096])[j]
        d1 = self.scr("wb_uq", [512, 3072], BF16)
        d2 = self.scr("wb_ukv", [512, 4096], BF16)
        so = Obj("wbs_mla")
        objs = []
        s1 = wuq.rearrange("k (h e) -> k h e", e=192)
        s2 = wukv.rearrange("k (h e) -> k h e", e=256)
        for q in range(4):
            r = slice(q * 128, (q + 1) * 128)
            for (dd, c0, c1, dw, ss, e0, e1) in ((d1, 0, 2048, 128, s1, 0, 128), (d1, 2048, 3072, 64, s1, 128, 192),
                                                 (d2, 0, 2048, 128, s2, 0, 128), (d2, 2048, 4096, 128, s2, 128, 256)):
                o = Obj("wb_mla")
                P.dma("gpsimd", dd[r, c0:c1].rearrange("k (h d) -> k h d", d=dw), ss[r, :, e0:e1], list(self.prep_gate), [o], so, bg=True)
                objs.append(o)
        self.wb["mla_u"] = (d1, d2, objs)

    def phase_mla(self, i, j, tiles):
        P, nc = self.P, self.nc
        V = nc.vector
        Wb, Wo = self.wb["mla_in"]
        d1, d2, uobjs = self.wb["mla_u"]
        g_q = self.inp("mla_g_q", [1, 512])
        g_kv = self.inp("mla_g_kv", [1, 512])
        QNT = self.scr("QNT", [16, 128, T], BF16)
        KNT = self.scr("KNT", [16, 128, T], BF16)
        QRT = self.scr("QRT", [8, 128, T], BF16)
        KRTd = self.scr("KRTd", [128, T], BF16)
        Vm = self.scr("Vm", [NT, 128, D], BF16)
        sts = self.supertiles(tiles)
        o_QNT = [[Obj() for _ in sts] for _ in range(16)]
        o_KNT = [[Obj() for _ in sts] for _ in range(16)]
        o_QRT = [Obj() for _ in range(NT)]
        o_KRT = [Obj() for _ in range(NT)]
        o_Vm = [Obj() for _ in range(NT)]
        P.begin_phase()
        C = self.load_consts(["ident", "cs"])
        ident, cs = C["ident"], C["cs"]
        Wuq = P.sb("Wuq", [128, 4, 3072], BF16)
        Wukv = P.sb("Wukv", [128, 4, 4096], BF16)
        P.dma("sync", Wuq[:], d1.rearrange("(kc p) n -> p kc n", p=128), list(uobjs), [Wuq.o], Wuq.o)
        P.dma("sync", Wukv[:], d2.rearrange("(kc p) n -> p kc n", p=128), list(uobjs), [Wukv.o], Wukv.o)
        gbc = [P.sb(f"gbc{q}", [128, 512], F32) for q in range(2)]
        for q, g in enumerate((g_q, g_kv)):
            P.dma("sync", gbc[q][:], g[j:j + 1, :].broadcast_to([128, 512]), [], [gbc[q].o], gbc[q].o)
            P.op("vector", lambda q=q: V.tensor_scalar_mul(out=gbc[q][:], in0=gbc[q][:], scalar1=math.sqrt(512.0)),
                 [gbc[q].o], [gbc[q].o])
        cT = [P.sb(f"cT{q}", [128, 4, 512], BF16) for q in range(2)]
        cnb = P.sb("cnb", [128, 512], BF16)
        krb = P.sb("krb", [128, 128], BF16)
        krs = P.sb("krs", [128, 128], BF16)
        tmps = (P.sb("rt1", [128, 256], F32), P.sb("rt2", [128, 256], F32))
        ss4 = P.sb("mss4", [128, 8], F32)
        ssum = P.sb("mssum", [128, 1], F32)
        rstd = P.sb("mrstd", [128, 1], F32)
        sq = P.sb("msq", [128, 512], BF16)
        ptc = P.ps("ptc", [128, 1024], BF16)
        ptq = P.ps("ptq", [128, 1024], BF16)
        stg = [P.sb(f"stg{q}", [128, 512], BF16) for q in range(3)]
        qrb = P.sb("qrb", [128, 1024], BF16)
        qrs = P.sb("qrs", [128, 8, 128], BF16)
        vst = [P.sb(f"vst{q}", [128, D], BF16) for q in range(1)]
        sti_of = {}
        for si, st in enumerate(sts):
            for t in st:
                sti_of[t] = si
        ctr = [0]
        ppsref = []

        def consume(st, tt, t, bi, pp):
            lat = t >= NTC
            n = t - NTC
            if bi < 2:
                self.norm_scale(pp[:, 0:512], [pp.o], 512, 1, ss4, ssum, rstd, sq, 512 * EPS)
                P.op("vector", lambda: V.scalar_tensor_tensor(out=cnb[:], in0=pp[:, 0:512], scalar=rstd[:, 0:1],
                                                              in1=gbc[bi][:], op0=ALU.mult, op1=ALU.mult),
                     [pp.o, rstd.o, gbc[bi].o], [cnb.o])
                for kc in range(4):
                    P.op("tensor", lambda kc=kc: nc.tensor.transpose(out=ptc[:, kc * 128:(kc + 1) * 128],
                                                                    in_=cnb[:, kc * 128:(kc + 1) * 128], identity=ident[:]),
                         [cnb.o, ident.o], [ptc.o])
                P.op("scalar", lambda: nc.scalar.activation(out=cT[bi][:, :, tt * 128:(tt + 1) * 128],
                                                            in_=ptc[:, 0:512].rearrange("p (k j) -> p k j", k=4), func=AF.Copy),
                     [ptc.o], [cT[bi].o])
            else:
                if lat:
                    X = pp[:, 0:64].rearrange("p (h a f) -> p h a f", h=1, a=2)
                    O = krb[:, 0:64].rearrange("p (h a f) -> p h a f", h=1, a=2)
                    self.rope(X, [pp.o], cs, n, O, krb.o, 1, tmps)
                else:
                    P.op("scalar", lambda: nc.scalar.activation(out=krb[:, 0:64], in_=pp[:, 0:64], func=AF.Copy), [pp.o], [krb.o])
                P.op("scalar", lambda: nc.scalar.activation(out=krb[:, 64:128], in_=krb[:, 0:64], func=AF.Copy), [krb.o], [krb.o])
                P.op("tensor", lambda: nc.tensor.transpose(out=ptc[:, 512:640], in_=krb[:], identity=ident[:]),
                     [krb.o, ident.o], [ptc.o])
                P.op("scalar", lambda: nc.scalar.activation(out=krs[:], in_=ptc[:, 512:640], func=AF.Copy), [ptc.o], [krs.o])
                P.dma("scalar", KRTd[:, t * 128:(t + 1) * 128], krs[:], [krs.o], [o_KRT[t]], krs.o)

        def st_end(st, aT):
            pps = ppsref[0]
            n = len(st) * 128
            t0 = st[0]
            si = sti_of[t0]
            for h in range(16):
                for which, (Wt, cq, dst, ob) in enumerate(((Wuq, cT[0], QNT, o_QNT), (Wukv, cT[1], KNT, o_KNT))):
                    pp = pps[ctr[0] % 2]
                    sg_ = stg[ctr[0] % 3]
                    ctr[0] += 1
                    for kc in range(4):
                        P.op("tensor", lambda pp=pp, Wt=Wt, cq=cq, kc=kc, h=h: nc.tensor.matmul(
                            pp[:, 0:n], lhsT=Wt[:, kc, h * 128:(h + 1) * 128], rhs=cq[:, kc, 0:n],
                            start=(kc == 0), stop=(kc == 3)), [Wt.o, cq.o], [pp.o])
                    P.op("scalar", lambda pp=pp, sg_=sg_: nc.scalar.activation(out=sg_[:, 0:n], in_=pp[:, 0:n], func=AF.Copy),
                         [pp.o], [sg_.o])
                    P.dma("scalar", dst[h, :, t0 * 128:t0 * 128 + n], sg_[:, 0:n], [sg_.o], [ob[h][si]], sg_.o)
            for tt, t in enumerate(st):
                lat = t >= NTC
                nl = t - NTC
                for half in range(2):
                    pp = pps[ctr[0] % 2]
                    ctr[0] += 1
                    for kc in range(4):
                        P.op("tensor", lambda pp=pp, kc=kc, tt=tt, half=half: nc.tensor.matmul(
                            pp[:, 0:512], lhsT=cT[0][:, kc, tt * 128:(tt + 1) * 128],
                            rhs=Wuq[:, kc, 2048 + half * 512:2048 + (half + 1) * 512], start=(kc == 0), stop=(kc == 3)),
                            [Wuq.o, cT[0].o], [pp.o])
                    if lat:
                        X = pp[:, 0:512].rearrange("p (h a f) -> p h a f", h=8, a=2)
                        O = qrb[:, half * 512:(half + 1) * 512].rearrange("p (h a f) -> p h a f", h=8, a=2)
                        self.rope(X, [pp.o], cs, nl, O, qrb.o, 8, tmps)
                    else:
                        P.op("scalar", lambda pp=pp, half=half: nc.scalar.activation(
                            out=qrb[:, half * 512:(half + 1) * 512], in_=pp[:, 0:512], func=AF.Copy), [pp.o], [qrb.o])
                for a in range(8):
                    P.op("tensor", lambda a=a: nc.tensor.transpose(out=ptq[:, a * 128:(a + 1) * 128],
                                                                  in_=qrb[:, a * 128:(a + 1) * 128], identity=ident[:]),
                         [qrb.o, ident.o], [ptq.o])
                P.op("scalar", lambda: nc.scalar.activation(out=qrs[:], in_=ptq[:].rearrange("p (a j) -> p a j", a=8),
                                                            func=AF.Copy), [ptq.o], [qrs.o])
                P.dma("scalar", QRT[:, :, t * 128:(t + 1) * 128].rearrange("a p j -> p a j"), qrs[:], [qrs.o], [o_QRT[t]], qrs.o)
                vs = vst[0]
                for nb in range(4):
                    pp = pps[ctr[0] % 2]
                    ctr[0] += 1
                    for kc in range(4):
                        P.op("tensor", lambda pp=pp, kc=kc, tt=tt, nb=nb: nc.tensor.matmul(
                            pp[:, 0:512], lhsT=cT[1][:, kc, tt * 128:(tt + 1) * 128],
                            rhs=Wukv[:, kc, 2048 + nb * 512:2048 + (nb + 1) * 512], start=(kc == 0), stop=(kc == 3)),
                            [Wukv.o, cT[1].o], [pp.o])
                    P.op("scalar", lambda pp=pp, vs=vs, nb=nb: nc.scalar.activation(
                        out=vs[:, nb * 512:(nb + 1) * 512], in_=pp[:, 0:512], func=AF.Copy), [pp.o], [vs.o])
                P.dma("scalar", Vm[t], vs[:], [vs.o], [o_Vm[t]], vs.o)

        blocks = [(0, 512), (512, 512), (1024, 64)]

        def st_begin(st, aT):
            pass

        orig_ps = P.ps
        made = []

        def ps_hook(name, shape, dtype=F32):
            t_ = orig_ps(name, shape, dtype)
            if name.startswith("pp"):
                made.append(t_)
                if len(made) == 2:
                    ppsref.append(made)
            return t_
        P.ps = ps_hook
        self.pa_loop(i, tiles, Wb, Wo, blocks, consume, per_st_end=st_end, extra_ps=4, C=C)
        P.ps = orig_ps
        P.end_phase()
        if self.stop == "pa":
            return
        P.begin_phase()
        C = self.load_consts(["ones_f128"])
        ones = C["ones_f128"]
        accs = [P.sb(f"acc{q}", [128, 512], F32) for q in range(4)]
        KRT = P.sb("KRT", [128, T], BF16)
        P.dma("sync", KRT[:], KRTd, list(o_KRT), [KRT.o], KRT.o)
        KN = [P.sb(f"KN{q}", [128, T], BF16) for q in range(2)]
        QN = [P.sb(f"QN{q}", [128, T], BF16) for q in range(2)]
        QR = [P.sb(f"QR{q}", [128, T], BF16) for q in range(2)]
        VH = [P.sb(f"VH{q}", [128, NT, 128], BF16) for q in range(2)]
        PT = [P.sb(f"PT{q}", [128, 512], BF16) for q in range(5)]
        rden = P.sb("rden", [128, 512], F32)
        otsb = P.sb("otsb", [128, 512], F32)
        ogs = [P.sb(f"ogs{q}", [128, 4, 128], BF16) for q in range(2)]
        Sps = [P.ps(f"S{q}", [128, 512], F32) for q in range(3)]
        OTp = [P.ps(f"OT{q}", [128, 512], F32) for q in range(2)]
        DNp = [P.ps(f"DN{q}", [128, 512], F32) for q in range(2)]
        OGT4 = self.OGT.rearrange("t p (k j) -> t p k j", k=KC)
        qsts = []
        if 0 in tiles:
            qsts.append((0, 2, [0, 1]))
        for s in range(8):
            qsts.append((NTC + s * 4, 4, list(range(NT))))
        qi = 0
        pi = 0
        for h in range(16):
            b = h % 2
            pb = (h % 2) * 64
            P.dma("sync", KN[b][:], KNT[h], list(o_KNT[h]), [KN[b].o], KN[b].o)
            P.dma("sync", QN[b][:], QNT[h], list(o_QNT[h]), [QN[b].o], QN[b].o)
            if h % 2 == 0:
                QRc = QR[(h // 2) % 2]
                P.dma("sync", QRc[:], QRT[h // 2], list(o_QRT), [QRc.o], QRc.o)
            P.dma("sync", VH[b][:], Vm[:, :, h * 128:(h + 1) * 128].rearrange("t p d -> p t d"), list(o_Vm),
                  [VH[b].o], VH[b].o)
            for (t0, nt, kts) in qsts:
                OT, DN = OTp[qi % 2], DNp[qi % 2]
                og = ogs[qi % 2]
                qi += 1
                pi = self._mla_q(h, KN[b], QN[b], QRc, VH[b], KRT, ones, pb, t0, nt, kts, OT, DN, og, Sps, PT, pi,
                                 rden, otsb, OGT4, (accs[(qi % 2) * 2], accs[(qi % 2) * 2 + 1]))
        P.end_phase()

    def _mla_q(self, h, KNb, QNb, QRc, VHb, KRT, ones, pb, t0, nt, kts, OT, DN, og, Sps, PT, pi, rden, otsb, OGT4, acc):
        P, nc = self.P, self.nc
        V = nc.vector
        n = nt * 128
        c0 = t0 * 128
        nk = len(kts)

        def emitS(kt, S):
            P.op("tensor", lambda: nc.tensor.matmul(S[:, 0:n], lhsT=KNb[:, kt * 128:(kt + 1) * 128],
                                                    rhs=QNb[:, c0:c0 + n], start=True, stop=False),
                 [KNb.o, QNb.o], [S.o])
            P.op("tensor", lambda: nc.tensor.matmul(S[:, 0:n], lhsT=KRT[pb:pb + 64, kt * 128:(kt + 1) * 128],
                                                    rhs=QRc[pb:pb + 64, c0:c0 + n], start=False, stop=True),
                 [KRT.o, QRc.o], [S.o])

        def pv(ki, kt, S, pt):
            P.op("scalar", lambda: nc.scalar.activation(out=pt[:, 0:n], in_=S[:, 0:n], func=AF.Exp, scale=MLA_SCALE),
                 [S.o], [pt.o])
            P.op("tensor", lambda: nc.tensor.matmul(OT[:, 0:n], lhsT=VHb[:, kt, :], rhs=pt[:, 0:n], start=(ki == 0),
                                                    stop=(ki == nk - 1)), [VHb.o, pt.o], [OT.o])
            ac = acc[ki % 2]
            if ki < 2:
                P.op("vector", lambda: V.tensor_copy(out=ac[:, 0:n], in_=pt[:, 0:n]), [pt.o], [ac.o])
            else:
                P.op("vector", lambda: V.tensor_tensor(out=ac[:, 0:n], in0=ac[:, 0:n], in1=pt[:, 0:n], op=ALU.add),
                     [pt.o, ac.o], [ac.o])
        emitS(kts[0], Sps[0])
        if nk > 1:
            emitS(kts[1], Sps[1])
        for ki, kt in enumerate(kts):
            if ki + 2 < nk:
                emitS(kts[ki + 2], Sps[(ki + 2) % 3])
            pv(ki, kt, Sps[ki % 3], PT[pi % 5])
            pi += 1
        P.op("tensor", lambda: nc.tensor.matmul(DN[:, 0:n], lhsT=ones[:], rhs=acc[0][:, 0:n], start=True, stop=False),
             [ones.o, acc[0].o], [DN.o])
        P.op("tensor", lambda: nc.tensor.matmul(DN[:, 0:n], lhsT=ones[:], rhs=acc[1][:, 0:n], start=False, stop=True),
             [ones.o, acc[1].o], [DN.o])
        P.op("scalar", lambda: nc.scalar.activation(out=rden[:, 0:n], in_=DN[:, 0:n], func=AF.Copy), [DN.o], [rden.o])
        P.op("scalar", lambda: nc.scalar.activation(out=otsb[:, 0:n], in_=OT[:, 0:n], func=AF.Copy), [OT.o], [otsb.o])
        P.op("vector", lambda: V.reciprocal(out=rden[:, 0:n], in_=rden[:, 0:n]), [rden.o], [rden.o])
        P.op("vector", lambda: V.tensor_tensor(out=og[:, 0:nt, :].rearrange("p t j -> p (t j)"), in0=otsb[:, 0:n],
                                               in1=rden[:, 0:n], op=ALU.mult), [otsb.o, rden.o], [og.o])
        P.dma("sync", OGT4[t0:t0 + nt, :, h, :].rearrange("t p j -> p t j"), og[:, 0:nt, :], [og.o],
              self.o_OGT[t0:t0 + nt], og.o)
        return pi

    def phase_gla(self, i, j, tiles_out):
        P, nc = self.P, self.nc
        V = nc.vector
        Wb, Wo = self.wb[f"gla_in{j}"]
        w_gd = self.inp("gla_w_gate_down", [2, 2, D, 16])[j]
        w_gu = self.inp("gla_w_gate_up", [2, 2, 16, 1024])[j]
        b_g = self.inp("gla_b_gate", [2, 2, 1024])[j]
        g_head = self.inp("gla_g_head", [2, 512])
        if not hasattr(self, "QET"):
            self.QET = self.scr("QET", [2, NT, 128, 1024], BF16)
            self.KET = self.scr("KET", [2, NT, 128, 1024], BF16)
            self.KD = self.scr("KD", [2, NT, 128, 1024], BF16)
            self.Vg = self.scr("Vg", [NT, 128, D], BF16)
            self.SR = self.scr("SR", [NT, 128, D], BF16)
            self.EBL = self.scr("EBL", [NT, 2, 128, 8], F32)
            self.OB = self.scr("OB", [NT, 128, D], F32)
            mk = lambda: [Obj() for _ in range(NT)]
            self.o_g = dict(qe=[mk(), mk()], ke=[mk(), mk()], kd=[mk(), mk()], v=mk(), sr=mk(), ebl=[mk(), mk()], ob=mk())
        QET, KET, KD, Vg, SR, EBL, OB, og_ = self.QET, self.KET, self.KD, self.Vg, self.SR, self.EBL, self.OB, self.o_g
        all_tiles = list(range(NT))
        P.begin_phase()
        C = self.load_consts(["ident", "tri"])
        ident, tri = C["ident"], C["tri"]
        Wgd = P.sb("Wgd", [128, KC, 32], BF16)
        wgs = P.sb("wgs", [128, 2, KC, 16], F32)
        for d in range(2):
            P.dma("sync", wgs[:, d, :, :], w_gd[d].rearrange("(kc p) r -> p kc r", p=128), [], [wgs.o], wgs.o)
        for d in range(2):
            P.op("scalar", lambda d=d: nc.scalar.activation(out=Wgd[:, :, d * 16:d * 16 + 16], in_=wgs[:, d, :, :],
                                                            func=AF.Copy), [wgs.o], [Wgd.o])
        WguA = [P.sb(f"WguA{d}", [17, 1024], F32) for d in range(2)]
        for d in range(2):
            P.dma("sync", WguA[d][0:16, :], w_gu[d], [], [WguA[d].o], WguA[d].o)
            P.dma("sync", WguA[d][16:17, :], b_g[d:d + 1, :], [], [WguA[d].o], WguA[d].o)
        gdTs = [P.sb(f"gdT{d}", [32, 512], F32) for d in range(2)]
        for d in range(2):
            P.op("vector", lambda d=d: V.memset(gdTs[d][:], 1.0), [], [gdTs[d].o])
        q_tm = [P.sb(f"q_tm{q}", [128, 1024], BF16) for q in range(4)]
        k_tm = [P.sb(f"k_tm{q}", [128, 1024], BF16) for q in range(4)]
        vst = [P.sb(f"vst{q}", [128, D], BF16) for q in range(4)]
        srst = [P.sb(f"srst{q}", [128, D], BF16) for q in range(4)]
        spf = P.sb("spf", [128, 1024], F32)
        e1 = P.sb("e1", [128, 1024], F32)
        e2 = P.sb("e2", [128, 1024], F32)
        e3 = P.sb("e3", [128, 1024], F32)
        ebl = [P.sb(f"ebl{q}", [128, 8], F32) for q in range(2)]
        sto = [P.sb(f"sto{q}", [128, 1024], BF16) for q in range(3)]
        X = P.ps("X", [128, 1024], F32)
        Y = P.ps("Y", [128, 1024], F32)
        ptr_ref = []
        sc = [0]
        LN16 = math.log(16.0)

        def consume(st, tt, t, bi, pp):
            if bi < 2:
                P.op("scalar", lambda: nc.scalar.activation(out=q_tm[tt][:, bi * 512:(bi + 1) * 512], in_=pp[:, 0:512],
                                                            func=AF.Copy), [pp.o], [q_tm[tt].o])
            elif bi < 4:
                P.op("scalar", lambda: nc.scalar.activation(out=k_tm[tt][:, (bi - 2) * 512:(bi - 1) * 512], in_=pp[:, 0:512],
                                                            func=AF.Copy), [pp.o], [k_tm[tt].o])
            elif bi < 8:
                P.op("scalar", lambda: nc.scalar.activation(out=vst[tt][:, (bi - 4) * 512:(bi - 3) * 512], in_=pp[:, 0:512],
                                                            func=AF.Copy), [pp.o], [vst[tt].o])
                if bi == 7:
                    P.dma("scalar", Vg[t], vst[tt][:], [vst[tt].o], [og_["v"][t]], vst[tt].o)
            else:
                P.op("scalar", lambda: nc.scalar.activation(out=srst[tt][:, (bi - 8) * 512:(bi - 7) * 512], in_=pp[:, 0:512],
                                                            func=AF.Silu), [pp.o], [srst[tt].o])
                if bi == 11:
                    P.dma("scalar", SR[t], srst[tt][:], [srst[tt].o], [og_["sr"][t]], srst[tt].o)

        import os
        GDBG = int(os.environ.get("GDBG", "9"))

        def st_begin(st, aT):
            n = len(st) * 128
            if GDBG < 2:
                return
            for d in range(2):
                for kc in range(KC):
                    P.op("tensor", lambda kc=kc, d=d: nc.tensor.matmul(X[0:16, d * 512:d * 512 + n],
                                                                      lhsT=Wgd[:, kc, d * 16:(d + 1) * 16], rhs=aT[:, kc, 0:n],
                                                                      start=(kc == 0), stop=(kc == KC - 1)), [Wgd.o, aT.o], [X.o])
                P.op("scalar", lambda d=d: nc.scalar.activation(out=gdTs[d][0:16, 0:n], in_=X[0:16, d * 512:d * 512 + n],
                                                                func=AF.Copy), [X.o], [gdTs[d].o])

        def gate_dir(tt, t, d, ptr):
            for half in range(2):
                hs = slice(half * 512, (half + 1) * 512)
                P.op("tensor", lambda hs=hs: nc.tensor.matmul(X[:, hs], lhsT=gdTs[d][0:17, tt * 128:(tt + 1) * 128],
                                                              rhs=WguA[d][0:17, hs], start=True, stop=True),
                     [gdTs[d].o, WguA[d].o], [X.o])
            if GDBG < 5:
                return
            P.op("scalar", lambda: nc.scalar.activation(out=spf[:], in_=X[:], func=AF.Exp, scale=-1.0), [X.o], [spf.o])
            P.op("scalar", lambda: nc.scalar.activation(out=spf[:], in_=spf[:], func=AF.Ln, bias=1.0), [spf.o], [spf.o])
            if GDBG < 6:
                return
            for c in range(8):
                P.op("tensor", lambda c=c: nc.tensor.matmul(Y[:, c * 128:(c + 1) * 128], lhsT=spf[:, c * 128:(c + 1) * 128],
                                                            rhs=tri[:, d, :], start=True, stop=True), [spf.o, tri.o], [Y.o])
            for half in range(2):
                hs = slice(half * 512, (half + 1) * 512)
                P.op("tensor", lambda hs=hs: nc.tensor.matmul(X[:, hs], lhsT=tri[:, 2 + d, :], rhs=spf[:, hs],
                                                              start=True, stop=True), [spf.o, tri.o], [X.o])
            if GDBG < 7:
                return
            P.op("scalar", lambda: nc.scalar.activation(out=e1[:], in_=Y[:], func=AF.Exp, bias=-LN16), [Y.o], [e1.o])
            P.op("scalar", lambda: nc.scalar.activation(out=e2[:], in_=Y[:], func=AF.Exp, scale=-1.0), [Y.o], [e2.o])
            P.op("scalar", lambda: nc.scalar.activation(out=e3[:], in_=X[:], func=AF.Exp), [X.o], [e3.o])
            eb = ebl[d]
            col = 127 if d == 0 else 0
            P.op("scalar", lambda: nc.scalar.activation(out=eb[:], in_=Y[:].rearrange("p (c i) -> p c i", c=8)[:, :, col],
                                                        func=AF.Exp), [Y.o], [eb.o])
            P.dma("scalar", EBL[t, d], eb[:], [eb.o], [og_["ebl"][d][t]], eb.o)
            if GDBG < 8:
                return
            for (src, ee, dst, ob) in ((ptr[:, 0:1024], e1, QET, og_["qe"]), (ptr[:, 1024:2048], e2, KET, og_["ke"]),
                                       (k_tm[tt][:], e3, KD, og_["kd"])):
                so_ = sto[sc[0] % 3]
                sc[0] += 1
                rd = [ptr.o, ee.o] if src is not k_tm[tt] else [k_tm[tt].o, ee.o]
                P.op("vector", lambda src=src, ee=ee, so_=so_: V.tensor_tensor(out=so_[:], in0=src, in1=ee[:], op=ALU.mult),
                     [ptr.o, k_tm[tt].o, ee.o], [so_.o])
                P.dma("sync", dst[d, t], so_[:], [so_.o], [ob[d][t]], so_.o)

        def st_end(st, aT):
            ptr = ptr_ref[0]
            if GDBG < 3:
                return
            for tt, t in enumerate(st):
                for c in range(8):
                    P.op("tensor", lambda c=c, tt=tt: nc.tensor.transpose(out=ptr[:, c * 128:(c + 1) * 128],
                                                                  in_=q_tm[tt][:, c * 128:(c + 1) * 128], identity=ident[:]),
                         [q_tm[tt].o, ident.o], [ptr.o])
                    P.op("tensor", lambda c=c, tt=tt: nc.tensor.transpose(out=ptr[:, 1024 + c * 128:1024 + (c + 1) * 128],
                                                                  in_=k_tm[tt][:, c * 128:(c + 1) * 128], identity=ident[:]),
                         [k_tm[tt].o, ident.o], [ptr.o])
                for d in range(2):
                    if GDBG >= 4:
                        gate_dir(tt, t, d, ptr)

        blocks = [(c * 512, 512) for c in range(12)]
        orig_ps = P.ps

        def ps_hook(name, shape, dtype=F32):
            t_ = orig_ps(name, shape, dtype)
            if name == "ptr":
                ptr_ref.append(t_)
            return t_
        P.ps = ps_hook
        self.pa_loop(i, all_tiles, Wb, Wo, blocks, consume, per_st_begin=st_begin, per_st_end=st_end, C=C)
        P.ps = orig_ps
        P.end_phase()
        if self.stop == "pa":
            return
        P.begin_phase()
        C = self.load_consts(["ident", "masks"])
        ident, masks = C["ident"], C["masks"]
        GH = P.sb("GH", [128, 512], F32)
        P.dma("sync", GH[:], g_head[j:j + 1, :].broadcast_to([128, 512]), [], [GH.o], GH.o)
        P.op("vector", lambda: V.tensor_scalar_mul(out=GH[:], in0=GH[:], scalar1=math.sqrt(512.0)), [GH.o], [GH.o])
        S = P.sb("S", [128, 4, 2, 512], F32)
        Sbf = P.sb("Sbf", [128, 4, 2, 512], BF16, nobj=8)
        Sob = [Obj() for _ in range(8)]
        B = dict(
            qe=[P.sb(f"qe{q}", [128, 1024], BF16) for q in range(2)], ke=[P.sb(f"ke{q}", [128, 1024], BF16) for q in range(2)],
            kd=[P.sb(f"kd{q}", [128, 1024], BF16) for q in range(2)], v=[P.sb(f"v{q}", [128, D], BF16) for q in range(2)],
            ebl=[P.sb(f"eb{q}", [128, 8], F32) for q in range(2)], ob=[P.sb(f"ob{q}", [128, D], F32) for q in range(2)],
            sr=[P.sb(f"sr{q}", [128, D], BF16) for q in range(2)], og=[P.sb(f"og{q}", [128, D], BF16) for q in range(2)],
            ogT=[P.sb(f"ogT{q}", [128, KC, 128], BF16) for q in range(2)],
            am=[P.sb(f"am{q}", [128, 128], BF16) for q in range(2)], ot=[P.sb(f"ot{q}", [128, 512], F32) for q in range(2)],
            t2=[P.sb(f"t2{q}", [128, 512], F32) for q in range(2)],
            ss4=P.sb("gss4", [128, 8], F32), ssum=P.sb("gssum", [128, 1], F32), rstd=P.sb("grstd", [128, 1], F32),
            sq=P.sb("gsq", [128, 512], BF16),
            o_ps=[P.ps(f"o{q}", [128, 512], F32) for q in range(2)], at_ps=P.ps("at", [128, 512], F32),
            pk=[P.ps(f"pk{q}", [128, 512], F32) for q in range(2)], ptr=P.ps("ptr", [128, D], BF16))
        outset = set(tiles_out)
        for d, order, fwd in ((1, [1, 0] + list(range(NT - 1, NTC - 1, -1)), False), (0, list(range(NT)), True)):
            P.op("vector", lambda: V.memset(S[:], 0.0), [], [S.o])
            P.op("vector", lambda: V.memset(Sbf[:], 0.0), [], list(Sbf.o))

            def loads(k, t):
                b = k % 2
                P.dma("sync", B["qe"][b][:], QET[d, t], [og_["qe"][d][t]], [B["qe"][b].o], B["qe"][b].o)
                P.dma("sync", B["ke"][b][:], KET[d, t], [og_["ke"][d][t]], [B["ke"][b].o], B["ke"][b].o)
                P.dma("sync", B["kd"][b][:], KD[d, t], [og_["kd"][d][t]], [B["kd"][b].o], B["kd"][b].o)
                P.dma("sync", B["v"][b][:], Vg[t], [og_["v"][t]], [B["v"][b].o], B["v"][b].o)
                P.dma("sync", B["ebl"][b][:], EBL[t, d], [og_["ebl"][d][t]], [B["ebl"][b].o], B["ebl"][b].o)
                if fwd and t in outset:
                    P.dma("sync", B["ob"][b][:], OB[t], [og_["ob"][t]], [B["ob"][b].o], B["ob"][b].o)
                    P.dma("sync", B["sr"][b][:], SR[t], [og_["sr"][t]], [B["sr"][b].o], B["sr"][b].o)
            loads(0, order[0])
            for k, t in enumerate(order):
                if k + 1 < len(order):
                    loads(k + 1, order[k + 1])
                need = t in outset
                for h in range(4):
                    self._gla_head(h, d, k, t, fwd, need, B, S, Sbf, masks, GH)
                b = k % 2
                if need and not fwd:
                    P.dma("scalar", OB[t], B["ob"][b][:], [B["ob"][b].o], [og_["ob"][t]], B["ob"][b].o)
                if need and fwd:
                    self._og_store(t, B["og"][b], B["ogT"][b], B["ptr"], ident)
        P.end_phase()

    def _og_store(self, t, og, oT, ptr, ident):
        P, nc = self.P, self.nc
        for kc in range(KC):
            P.op("tensor", lambda kc=kc: nc.tensor.transpose(out=ptr[:, kc * 128:(kc + 1) * 128],
                                                            in_=og[:, kc * 128:(kc + 1) * 128], identity=ident[:]),
                 [og.o, ident.o], [ptr.o])
        P.op("scalar", lambda: nc.scalar.activation(out=oT[:], in_=ptr[:].rearrange("p (k j) -> p k j", k=KC), func=AF.Copy),
             [ptr.o], [oT.o])
        P.dma("scalar", self.OGT[t], oT[:].rearrange("p k j -> p (k j)"), [oT.o], [self.o_OGT[t]], oT.o)

    def _gla_head(self, h, d, k, t, fwd, need, B, S, Sbf, masks, GH):
        P, nc = self.P, self.nc
        V = nc.vector
        b = k % 2
        qe, ke, kd, v, ebl = B["qe"][b], B["ke"][b], B["kd"][b], B["v"][b], B["ebl"][b]
        o_ps = B["o_ps"][(k * 4 + h) % 2]
        at_ps = B["at_ps"]
        am = B["am"][(k * 4 + h) % 2]
        vs = slice(h * 512, (h + 1) * 512)
        if need:
            for kc in range(2):
                c = h * 2 + kc
                P.op("tensor", lambda kc=kc, c=c: nc.tensor.matmul(o_ps[:, 0:512], lhsT=qe[:, c * 128:(c + 1) * 128],
                                                                  rhs=Sbf[:, h, kc, :], start=(kc == 0), stop=False),
                     [qe.o, Sbf.o[c]], [o_ps.o])
            for kc in range(2):
                c = h * 2 + kc
                P.op("tensor", lambda kc=kc, c=c: nc.tensor.matmul(at_ps[:, 0:128], lhsT=ke[:, c * 128:(c + 1) * 128],
                                                                  rhs=qe[:, c * 128:(c + 1) * 128], start=(kc == 0),
                                                                  stop=(kc == 1)), [ke.o, qe.o], [at_ps.o])
            P.op("vector", lambda: V.tensor_tensor(out=am[:], in0=at_ps[:, 0:128], in1=masks[:, d, :], op=ALU.mult),
                 [at_ps.o, masks.o], [am.o])
            P.op("tensor", lambda: nc.tensor.matmul(o_ps[:, 0:512], lhsT=am[:], rhs=v[:, vs], start=False, stop=True),
                 [am.o, v.o], [o_ps.o])
        for kc in range(2):
            c = h * 2 + kc
            pk = B["pk"][kc]
            P.op("tensor", lambda c=c, pk=pk: nc.tensor.matmul(pk[:, 0:512], lhsT=kd[:, c * 128:(c + 1) * 128], rhs=v[:, vs],
                                                              start=True, stop=True), [kd.o, v.o], [pk.o])
            P.op("vector", lambda kc=kc, c=c, pk=pk: V.scalar_tensor_tensor(
                out=S[:, h, kc, :], in0=S[:, h, kc, :], scalar=ebl[:, c:c + 1], in1=pk[:, 0:512], op0=ALU.mult, op1=ALU.add),
                [S.o, ebl.o, pk.o], [S.o])
            P.op("scalar", lambda kc=kc: nc.scalar.activation(out=Sbf[:, h, kc, :], in_=S[:, h, kc, :], func=AF.Copy),
                 [S.o], [Sbf.o[c]])
        if not need:
            return
        if not fwd:
            ob = B["ob"][b]
            P.op("scalar", lambda: nc.scalar.activation(out=ob[:, vs], in_=o_ps[:, 0:512], func=AF.Copy), [o_ps.o], [ob.o])
            return
        ob, sr, og = B["ob"][b], B["sr"][b], B["og"][b]
        ot = B["ot"][h % 2]
        t2 = B["t2"][h % 2]
        P.op("vector", lambda: V.tensor_tensor(out=ot[:], in0=o_ps[:, 0:512], in1=ob[:, vs], op=ALU.add), [o_ps.o, ob.o], [ot.o])
        self.norm_scale(ot[:], [ot.o], 512, 1, B["ss4"], B["ssum"], B["rstd"], B["sq"], 512 * EPS)
        P.op("vector", lambda: V.scalar_tensor_tensor(out=t2[:], in0=ot[:], scalar=B["rstd"][:, 0:1], in1=GH[:],
                                                      op0=ALU.mult, op1=ALU.mult), [ot.o, B["rstd"].o, GH.o], [t2.o])
        P.op("vector", lambda: V.tensor_tensor(out=og[:, vs], in0=t2[:], in1=sr[:, vs], op=ALU.mult), [t2.o, sr.o], [og.o])

    def build(self, skip_ffn=False, stop=None):
        P = self.P
        self.stop = stop
        def prep(i):
            kind, j = i % 3, i // 3
            if kind == 0:
                self.prep_nat(f"gla_in{j}", self.inp("gla_w_in", [2, D, 6144])[j], [D, 6144], pieces=8)
                self.prep_nat(f"gla_out{j}", self.inp("gla_w_out", [2, D, D])[j], [D, D])
            elif kind == 1:
                self.prep_mla(j)
            else:
                self.prep_nat("swa_in", self.inp("swa_w_in", [1, D, 2560])[j], [D, 2560])
                self.prep_nat("swa_out", self.inp("swa_w_out", [1, D, D])[j], [D, D])
            if not skip_ffn:
                self.prep_ffn(i)
        PREPALL = True
        for i_ in (self.layers if PREPALL else self.layers[:1]):
            prep(i_)
        self.phase_init()
        self.phase_mod(self.layers)
        lat_tiles = list(range(NTC, NT))
        all_tiles = list(range(NT))
        for li, i in enumerate(self.layers):
            if li + 1 < len(self.layers) and not PREPALL:
                self.prep_gate = [self.o_modv[i]] if os.environ.get("NOGATE") is None else []
                prep(self.layers[li + 1])
            if stop == "mod":
                break
            kind, j = i % 3, i // 3
            last = (i == self.final_layer)
            tiles_out = lat_tiles if last else all_tiles
            if kind == 0:
                self.phase_gla(i, j, tiles_out)
                wo = self.wb[f"gla_out{j}"]
            elif kind == 1:
                self.phase_mla(i, j, tiles_out)
                wo = self.wb["mla_out"]
            else:
                self.phase_swa(i, j, all_tiles)
                wo = self.wb["swa_out"]
            if stop in ("mix", "pa"):
                break
            self.phase_post(i, wo[0], wo[1], tiles_out)
            if not skip_ffn:
                self.phase_ffn(i, tiles_out, to_out=last)
        P.barrier()
        nc = P.finish()
        P.root.close()
        return nc


def _prep_inputs(inputs, b, names):
    f = np.ascontiguousarray
    m = {}
    cst = _consts()
    for k in names:
        if k == "x":
            m[k] = f(inputs["x"][b])
        elif k == "ctx":
            m[k] = f(inputs["ctx"][b])
        elif k == "cc":
            m[k] = f(np.stack([inputs["c"][b], inputs["c_ctx"]]))
        elif k in cst:
            m[k] = cst[k]
        else:
            m[k] = f(inputs[k])
    return m


def run(inputs, layers=(0, 1, 2, 3), debug_outs=(), final_layer=DEPTH - 1, cores=8, skip_ffn=False, trace=False,
        stop=None):
    bld = Builder(layers=layers, debug_outs=debug_outs, final_layer=final_layer)
    nc = bld.build(skip_ffn=skip_ffn, stop=stop)
    print("stats", bld.P.stats, flush=True)
    names = list(bld.in_decl.keys())
    in_maps = [_prep_inputs(inputs, b, names) for b in range(cores)]
    res = run_bass_kernel_spmd(nc, in_maps, core_ids=list(range(cores)), trace=trace)
    return res


def kernel(**inputs):
    inputs = {k: np.asarray(v) for k, v in inputs.items()}
    res = run(inputs)
    return np.stack([np.asarray(r["out"]) for r in res.results], axis=0).astype(np.float32)
```
